# Optimizing a Trainium2 kernel written in Bass

```python
import math
import jax, jax.numpy as jnp
from jax import lax
import numpy as np

D_MODEL = 1024
BATCH = 16
SEQ = 2048
DEPTH = 1

MLSTM_HEADS = 4
MLSTM_HEAD_DIM = 256
MLSTM_WIDTH = MLSTM_HEADS * MLSTM_HEAD_DIM
CONV_WIDTH = 4
CHUNK = 64
POOL_WINDOWS = (2, 4, 8, 16)
POOL_GROUPS = len(POOL_WINDOWS)
POOL_GROUP_DIM = 128
POOL_WIDTH = POOL_GROUPS * POOL_GROUP_DIM
N_BRANCHES = 2
D_FF = int(math.ceil((8 * D_MODEL / 3) / 256) * 256)
EPS = 1e-6
SPLIT_SIZES = (2 * MLSTM_WIDTH, MLSTM_WIDTH, MLSTM_WIDTH, MLSTM_HEADS, MLSTM_HEADS, POOL_WIDTH, N_BRANCHES * D_MODEL)
N_IN = sum(SPLIT_SIZES)

kernel_name = "hybrid_mlstm_pool_gated_block"


def rms_norm(x, g):
    xf = x.astype(jnp.float32)
    y = xf * lax.rsqrt(jnp.mean(xf * xf, axis=-1, keepdims=True) + EPS)
    return (y * g.astype(jnp.float32)).astype(x.dtype)


def causal_depthwise_conv(u, w):
    c = u.shape[-1]
    return lax.conv_general_dilated(
        u, w[:, None, :].astype(u.dtype), window_strides=(1,), padding=[(CONV_WIDTH - 1, 0)],
        dimension_numbers=("NWC", "WIO", "NWC"), feature_group_count=c)


def mlstm_chunkwise(q, k, v, log_i, log_f):
    b_, s_, h_, dh = q.shape
    nc = s_ // CHUNK

    def to_chunks(t):
        return t.reshape(b_, nc, CHUNK, h_, -1).transpose(1, 0, 3, 2, 4)

    def gate_chunks(t):
        return t.reshape(b_, nc, CHUNK, h_).transpose(1, 0, 3, 2)

    causal = jnp.tril(jnp.ones((CHUNK, CHUNK), dtype=bool))

    def step(carry, xs):
        c_state, n_state, m_prev = carry
        qc, kc, vc, li, lf = xs
        bcum = jnp.cumsum(lf, axis=-1)
        dmat = jnp.where(causal, bcum[..., :, None] - bcum[..., None, :] + li[..., None, :], -jnp.inf)
        inter = bcum + m_prev[..., None]
        m = jnp.maximum(inter, jnp.max(dmat, axis=-1))
        dexp = jnp.exp(dmat - m[..., None])
        w_inter = jnp.exp(inter - m)
        s = jnp.einsum('bhld,bhsd->bhls', qc, kc) * dexp
        num = w_inter[..., None] * jnp.einsum('bhvk,bhlk->bhlv', c_state, qc) + jnp.einsum('bhls,bhsv->bhlv', s, vc)
        den = w_inter * jnp.einsum('bhk,bhlk->bhl', n_state, qc) + jnp.sum(s, axis=-1)
        h = num / jnp.maximum(jnp.abs(den), jnp.exp(-m))[..., None]
        b_last = bcum[..., -1]
        g = b_last[..., None] - bcum + li
        m_new = jnp.maximum(b_last + m_prev, jnp.max(g, axis=-1))
        w = jnp.exp(g - m_new[..., None])
        decay = jnp.exp(b_last + m_prev - m_new)
        c_state = decay[..., None, None] * c_state + jnp.einsum('bhsv,bhsk->bhvk', vc * w[..., None], kc)
        n_state = decay[..., None] * n_state + jnp.einsum('bhs,bhsk->bhk', w, kc)
        return (c_state, n_state, m_new), h

    init = (jnp.zeros((b_, h_, dh, dh), jnp.float32), jnp.zeros((b_, h_, dh), jnp.float32),
            jnp.zeros((b_, h_), jnp.float32))
    xs = (to_chunks(q), to_chunks(k), to_chunks(v), gate_chunks(log_i), gate_chunks(log_f))
    _, h = lax.scan(step, init, xs)
    return h.transpose(1, 0, 3, 2, 4).reshape(b_, s_, h_, dh)


def pool_mixer(p, w_pool, pool_scale):
    b_, s_, _ = p.shape
    pg = p.reshape(b_, s_, POOL_GROUPS, POOL_GROUP_DIM).astype(jnp.float32)
    cs = jnp.cumsum(pg, axis=1)
    t = jnp.arange(s_)
    outs = []
    for gi, w in enumerate(POOL_WINDOWS):
        csg = cs[:, :, gi]
        shifted = jnp.pad(csg, ((0, 0), (w, 0), (0, 0)))[:, :s_]
        cnt = jnp.minimum(t + 1, w).astype(jnp.float32)[None, :, None]
        outs.append((csg - shifted) / cnt - pg[:, :, gi])
    pooled = jnp.stack(outs, axis=2).astype(p.dtype)
    y = jnp.einsum('bsgc,gcd->bsgd', pooled, w_pool).reshape(b_, s_, POOL_WIDTH)
    return y * pool_scale


def setup_inputs(seed: int = 0) -> dict:
    key = jax.random.key(seed)
    ks = jax.random.split(key, 20)
    nrm = lambda k, shape, scale: jax.random.normal(k, shape, jnp.float32) * scale
    return {
        "x": nrm(ks[0], (BATCH, SEQ, D_MODEL), 1.0),
        "norm1_g": 1.0 + nrm(ks[1], (DEPTH, D_MODEL), 0.02),
        "w_in": nrm(ks[2], (DEPTH, D_MODEL, N_IN), D_MODEL ** -0.5),
        "conv_qk": nrm(ks[3], (DEPTH, CONV_WIDTH, 2 * MLSTM_WIDTH), CONV_WIDTH ** -0.5),
        "b_igate": nrm(ks[4], (DEPTH, MLSTM_HEADS), 0.1),
        "b_fgate": jnp.tile(jnp.linspace(3.0, 6.0, MLSTM_HEADS, dtype=jnp.float32)[None], (DEPTH, 1))
                   + nrm(ks[5], (DEPTH, MLSTM_HEADS), 0.1),
        "mh_norm_g": 1.0 + nrm(ks[6], (DEPTH, MLSTM_WIDTH), 0.02),
        "w_pool": nrm(ks[7], (DEPTH, POOL_GROUPS, POOL_GROUP_DIM, POOL_GROUP_DIM), POOL_GROUP_DIM ** -0.5),
        "pool_scale": 1.0 + nrm(ks[8], (DEPTH, POOL_WIDTH), 0.1),
        "w_branch_a": nrm(ks[9], (DEPTH, MLSTM_WIDTH, D_MODEL), MLSTM_WIDTH ** -0.5),
        "w_branch_b": nrm(ks[10], (DEPTH, POOL_WIDTH, D_MODEL), POOL_WIDTH ** -0.5),
        "b_gate": nrm(ks[11], (DEPTH, N_BRANCHES * D_MODEL), 0.01),
        "w_out": nrm(ks[12], (DEPTH, D_MODEL, D_MODEL), D_MODEL ** -0.5),
        "norm2_g": 1.0 + nrm(ks[13], (DEPTH, D_MODEL), 0.02),
        "w_ffn_gate": nrm(ks[14], (DEPTH, D_MODEL, D_FF), D_MODEL ** -0.5),
        "w_ffn_up": nrm(ks[15], (DEPTH, D_MODEL, D_FF), D_MODEL ** -0.5),
        "w_ffn_down": nrm(ks[16], (DEPTH, D_FF, D_MODEL), D_FF ** -0.5),
        "final_norm_g": 1.0 + nrm(ks[17], (D_MODEL,), 0.02),
    }


def reference(x, norm1_g, w_in, conv_qk, b_igate, b_fgate, mh_norm_g, w_pool, pool_scale,
              w_branch_a, w_branch_b, b_gate, w_out, norm2_g, w_ffn_gate, w_ffn_up, w_ffn_down,
              final_norm_g):
    b_, s_, _ = x.shape
    split_idx = list(np.cumsum(SPLIT_SIZES)[:-1])
    for l in range(DEPTH):
        h = rms_norm(x, norm1_g[l])
        z = h @ w_in[l]
        qk_pre, v, o_pre, i_pre, f_pre, pool_in, gate_pre = jnp.split(z, split_idx, axis=-1)

        qk = jax.nn.silu(causal_depthwise_conv(qk_pre, conv_qk[l]))
        q, k = jnp.split(qk, 2, axis=-1)
        heads = lambda t: t.reshape(b_, s_, MLSTM_HEADS, MLSTM_HEAD_DIM).astype(jnp.float32)
        qh = heads(q) * (MLSTM_HEAD_DIM ** -0.5)
        kh, vh = heads(k), heads(v)
        log_i = (i_pre + b_igate[l]).astype(jnp.float32)
        log_f = jax.nn.log_sigmoid((f_pre + b_fgate[l]).astype(jnp.float32))
        hm = mlstm_chunkwise(qh, kh, vh, log_i, log_f)
        hm = hm * lax.rsqrt(jnp.mean(hm * hm, axis=-1, keepdims=True) + EPS)
        hm = hm.reshape(b_, s_, MLSTM_WIDTH).astype(x.dtype) * mh_norm_g[l]
        y_a = (jax.nn.sigmoid(o_pre) * hm) @ w_branch_a[l]

        y_b = pool_mixer(pool_in, w_pool[l], pool_scale[l]) @ w_branch_b[l]

        g_a, g_b = jnp.split(jax.nn.sigmoid(gate_pre + b_gate[l]), N_BRANCHES, axis=-1)
        x = x + (g_a * y_a + g_b * y_b) @ w_out[l]

        h2 = rms_norm(x, norm2_g[l])
        x = x + (jax.nn.silu(h2 @ w_ffn_gate[l]) * (h2 @ w_ffn_up[l])) @ w_ffn_down[l]
    return rms_norm(x, final_norm_g)
```

```python
import numpy as np
from contextlib import ExitStack
import concourse.bass as bass
import concourse.mybir as mybir
from concourse.bass_utils import run_bass_kernel_spmd

F32 = mybir.dt.float32
BF16 = mybir.dt.bfloat16
AF = mybir.ActivationFunctionType
ALU = mybir.AluOpType

FULL = dict(D=1024, H=4, DH=256, S=2048, NSEQ=2, DFF=2816, T=512)
EPS = 1e-6
NSLOT = 4


class Prog:
    ENGS = ("pe", "act", "dve", "pool", "sp")

    def __init__(self, nc, stack, self_wait=True):
        self.nc, self.stack, self.self_wait = nc, stack, self_wait
        self.ops = {e: [] for e in self.ENGS}
        self.cnt, self.sems = {}, {}
        for e in self.ENGS:
            self.sems[e] = stack.enter_context(nc.semaphore("prog_" + e))
            self.cnt[e] = 0
        self.last_w, self.readers = {}, {}
        self.waited = {e: {} for e in self.ENGS}
        self.dma_keys = {}

    def dma_sem(self, name):
        self.sems[name] = self.stack.enter_context(self.nc.semaphore(name))
        self.cnt[name] = 0
        return name

    def op(self, eng, fn, reads=(), writes=(), dsem=None, ndma=1, store=False):
        deps = []
        for b in reads:
            if b in self.last_w:
                deps.append(self.last_w[b])
        for b in writes:
            if b in self.last_w:
                deps.append(self.last_w[b])
            deps += self.readers.get(b, [])
        if dsem is None:
            self.cnt[eng] += 1
            tok = (eng, self.cnt[eng])
        else:
            keyset = tuple(sorted(map(str, reads if store else writes)))
            prev = self.dma_keys.setdefault(dsem, keyset)
            assert prev == keyset, f"DMA sem {dsem} reused for different buffers: {prev} vs {keyset}"
            self.cnt[dsem] += 16 * ndma
            tok = (dsem, self.cnt[dsem])
        waits = {}
        for (s, v) in deps:
            if s not in self.ENGS:
                assert v == self.cnt[s] or (s == dsem and v == self.cnt[s] - 16 * ndma), \
                    f"dep on stale DMA token ({s},{v}) latest={self.cnt[s]}"
            if s == eng and (eng == "pe" or not self.self_wait):
                continue
            if self.waited[eng].get(s, 0) >= v:
                continue
            if waits.get(s, 0) < v:
                waits[s] = v
        for s, v in waits.items():
            self.waited[eng][s] = v
        self.ops[eng].append((waits, fn, tok, dsem is not None))
        for b in reads:
            self.readers.setdefault(b, []).append(tok)
        for b in writes:
            self.last_w[b] = tok
            self.readers[b] = []
        return tok

    def final_wait(self, eng, toks):
        waits = {}
        for (s, v) in toks:
            waits[s] = max(waits.get(s, 0), v)
        self.ops[eng].append((waits, None, None, False))

    def emit(self):
        nc = self.nc
        with nc.Block() as block:
            def run(engname):
                def body(e):
                    for (waits, fn, tok, isdma) in self.ops[engname]:
                        for s, v in waits.items():
                            e.wait_ge(self.sems[s], v)
                        if fn is None:
                            continue
                        r = fn(e)
                        if isdma:
                            for ins in (r if isinstance(r, (list, tuple)) else [r]):
                                ins.then_inc(self.sems[tok[0]], 16)
                        else:
                            if isinstance(r, (list, tuple)):
                                r = r[-1]
                            r.then_inc(self.sems[tok[0]], 1)
                return body
            block.sync(run("sp"))
            block.tensor(run("pe"))
            block.scalar(run("act"))
            block.vector(run("dve"))
            block.gpsimd(run("pool"))


def derived(cfg):
    c = dict(cfg)
    D, H, DH, S, NSEQ, DFF, T = (cfg[k] for k in ("D", "H", "DH", "S", "NSEQ", "DFF", "T"))
    c.update(DT=D // 128, KT=DH // 128, MW=H * DH, MT=H * DH // 128, FT=DFF // 128, NCH=T // 128,
             TPS=S // T, NT=NSEQ * S // T, G=4, PW=512)
    c["NIN"] = 4 * c["MW"] + 2 * H + 512 + 2 * D
    c["CWD"] = min(512, D)
    c["NHD"] = D // c["CWD"]
    c["CWF"] = min(512, DFF)
    c["NPR"] = 4 * 2 * c["MT"] + 2 * c["DT"] + 4
    return c


def build(cfg):
    c = derived(cfg)
    D, H, DH, S, NSEQ, DFF, T = (c[k] for k in ("D", "H", "DH", "S", "NSEQ", "DFF", "T"))
    DT, KT, MW, MT, FT, NCH, TPS, NT, G, NIN, CWD, NHD, CWF, NPR = (
        c[k] for k in ("DT", "KT", "MW", "MT", "FT", "NCH", "TPS", "NT", "G", "NIN", "CWD", "NHD", "CWF", "NPR"))
    TPC = CWD // 128
    assert MW % 512 == 0 and D % CWD == 0 and DT <= 8 and MT <= 8 and NPR <= 128
    nc = bass.Bass("TRN2", target_bir_lowering=False)
    dr = lambda n, s, d, k: nc.dram_tensor(n, s, d, kind=k).ap()
    x_d = dr("x", [NSEQ * S, D], F32, "ExternalInput")
    out_d = dr("out", [NSEQ * S, D], F32, "ExternalOutput")
    w_in_d = dr("w_in", [D, NIN], F32, "ExternalInput")
    w_pool_d = dr("w_pool", [4, 128, 128], F32, "ExternalInput")
    w_a_d = dr("w_a", [MW, D], F32, "ExternalInput")
    w_b_d = dr("w_b", [512, D], F32, "ExternalInput")
    w_out_d = dr("w_out", [D, D], F32, "ExternalInput")
    wg_d = dr("wg", [D, DFF], F32, "ExternalInput")
    wu_d = dr("wu", [D, DFF], F32, "ExternalInput")
    wd_d = dr("wd", [DFF, D], F32, "ExternalInput")
    g1_d = dr("g1", [1, D], F32, "ExternalInput")
    g2_d = dr("g2", [1, D], F32, "ExternalInput")
    gf_d = dr("gf", [1, D], F32, "ExternalInput")
    mhg_d = dr("mhg", [1, MW], F32, "ExternalInput")
    prm_d = dr("prm", [NPR, 128], F32, "ExternalInput")
    gb_d = dr("gb", [1, 128], F32, "ExternalInput")

    chunks = []

    def add_chunk(name, src, a, b):
        chunks.append((name, src, a, b))

    def colv(w, c0, c1):
        return w[:, c0:c1].rearrange("(k p) c -> p k c", p=128)

    o_qk, o_v, o_o, o_g, o_gate = 0, 2 * MW, 3 * MW, 4 * MW, 4 * MW + 2 * H + 512
    GPW = 2 * H + 512
    add_chunk("gp", colv(w_in_d, o_g, o_g + GPW), DT, GPW)
    NQK = 2 * MW // 512
    NV = MW // 512
    qk_order = list(range(NQK // 2, NQK)) + list(range(NQK // 2))
    vo_list = [("v", i) for i in range(NV)] + [("o", i) for i in range(NV)]
    assert len(vo_list) == NQK
    for p_ in range(NQK):
        i = qk_order[p_]
        add_chunk(f"qk{i}", colv(w_in_d, o_qk + i * 512, o_qk + (i + 1) * 512), DT, 512)
        kind, vi = vo_list[p_]
        off = o_v if kind == "v" else o_o
        add_chunk(f"{kind}{vi}", colv(w_in_d, off + vi * 512, off + (vi + 1) * 512), DT, 512)
    add_chunk("wpool", w_pool_d.rearrange("g c d -> c g d"), 4, 128)
    for hf in range(NHD):
        add_chunk(f"wa{hf}", colv(w_a_d, hf * CWD, (hf + 1) * CWD), MT, CWD)
        add_chunk(f"wb{hf}", colv(w_b_d, hf * CWD, (hf + 1) * CWD), 4, CWD)
        add_chunk(f"ga{hf}", colv(w_in_d, o_gate + hf * CWD, o_gate + (hf + 1) * CWD), DT, CWD)
        add_chunk(f"gb{hf}", colv(w_in_d, o_gate + D + hf * CWD, o_gate + D + (hf + 1) * CWD), DT, CWD)
    for hf in range(NHD):
        add_chunk(f"wo{hf}", colv(w_out_d, hf * CWD, (hf + 1) * CWD), DT, CWD)
    fch = []
    f0 = 0
    while f0 < FT:
        n = min(CWF // 128, FT - f0)
        fch.append((f0, n))
        f0 += n
    for i, (f0, n) in enumerate(fch):
        add_chunk(f"wg{i}", colv(wg_d, f0 * 128, (f0 + n) * 128), DT, n * 128)
        add_chunk(f"wu{i}", colv(wu_d, f0 * 128, (f0 + n) * 128), DT, n * 128)
    KC2 = max(1, 4096 // CWD)
    dch = []
    f0 = 0
    while f0 < FT:
        n = min(KC2, FT - f0)
        dch.append((f0, n))
        f0 += n
    for hf in range(NHD):
        for i, (f0, n) in enumerate(dch):
            add_chunk(f"wd{hf}_{i}", wd_d[f0 * 128:(f0 + n) * 128, hf * CWD:(hf + 1) * CWD].rearrange("(k p) c -> p k c", p=128), n, CWD)
    NCK = len(chunks)
    SLOT = max(a * b for (_, _, a, b) in chunks)
    cidx = {nm: i for i, (nm, _, _, _) in enumerate(chunks)}
    scr = [dr(f"sc_{nm}", [128, a * b], BF16, "Internal") for (nm, _, a, b) in chunks]

    with ExitStack() as st:
        P = Prog(nc, st)
        sb = lambda n, s, d: st.enter_context(nc.sbuf_tensor(n, s, d))
        ps = lambda n, s, d: st.enter_context(nc.psum_tensor(n, s, d))

        ident_f = sb("ident_f", [128, 128], F32)
        ident_b = sb("ident_b", [128, 128], BF16)
        tri = sb("tri", [128, 128], F32)
        epsb = sb("epsb", [128, 1], F32)
        g_bc = sb("g_bc", [128, D], F32)
        mhg_bc = sb("mhg_bc", [128, MW], F32)
        gbias = sb("gbias", [128, 128], F32)
        prm_r = sb("prm_r", [128, 128], F32)
        pcol = sb("pcol", [128, 128], F32)
        rc15 = sb("rc15", [128, G, 15], F32)
        xts = [sb(f"xt{i}", [128, NCH, D], F32) for i in range(2)]
        sqj = sb("sqj", [128, D], BF16)
        xn = sb("xn", [128, NCH, max(D, MW)], BF16)
        hT = sb("hT", [128, DT, T], BF16)
        hmT = sb("hmT", [128, MT, T], BF16)
        qT = sb("qT", [128, MT, T], BF16)
        kT = sb("kT", [128, MT, T], BF16)
        AB = sb("AB", [128, NCH, H, 128], F32)
        vaug = sb("vaug", [128, NCH, H, DH + 1], BF16)
        ktok = sb("ktok", [128, NCH, H, DH], BF16)
        osig = sb("osig", [128, NCH, MW], BF16)
        gates = sb("gates", [128, NCH, 2 * H], F32)
        nlf = sb("nlf", [128, NCH, H], F32)
        ebl = sb("ebl", [128, NCH, H], F32)
        ebd = sb("ebd", [128, NCH, H], F32)
        ssq = sb("ssq", [128, NCH], F32)
        rstd = sb("rstd", [128, NCH], F32)
        cbuf = [sb(f"cbuf{i}", [128, T + 3], F32) for i in range(2)]
        cacc = [sb(f"cacc{i}", [128, T], F32) for i in range(2)]
        csil = [sb("csil0", [128, T], F32)] * 2
        chalo = sb("chalo", [128, 2 * MT, 3], F32)
        praw = sb("praw", [128, G, 15 + T], F32)
        ptmp = [sb(f"ptmp{i}", [128, 15 + T], F32) for i in range(2)]
        phalo = sb("phalo", [128, G, 15], F32)
        pooled = sb("pooled", [128, G, T], BF16)
        ysT = sb("ysT", [128, G, T], BF16)
        PT = [sb(f"PT{i}", [128, H, 128], BF16) for i in range(2)]
        Cst = sb("Cst", [128, H, KT, DH + 1], F32)
        Cbf = [sb(f"Cbf{i}", [128, H, KT, DH + 1], BF16) for i in range(2)]
        hsm = [sb(f"hsm{i}", [128, 8], F32) for i in range(2)]
        hjunk = [sb("hjunk0", [128, DH], BF16)] * 2
        htmp = [sb(f"htmp{i}", [128, DH], F32) for i in range(2)]
        assert DT * T <= NCH * H * DH and FT <= 3 * MT
        mixT = ktok[:].rearrange("p c h d -> p (c h d)")[:, 0:DT * T].rearrange("p (k t) -> p k t", k=DT)
        MIXK = "ktok"
        sga = [sb("sga0", [128, T], F32)] * 2
        sgb = [sb("sgb0", [128, T], F32)] * 2
        fsg = [sb("fsg0", [128, T], F32)] * 2
        slots = [sb(f"wslot{i}", [128, SLOT], BF16) for i in range(NSLOT)]

        def actT(f):
            if f < MT:
                return qT[:, f, :], "qT"
            if f < 2 * MT:
                return kT[:, f - MT, :], "kT"
            return hmT[:, f - 2 * MT, :], "hmT"

        pF = [ps(f"pf{i}", [128, 512], F32) for i in range(6)]
        pT = [ps(f"pt{i}", [128, 8, 128], BF16) for i in range(2)]
        rot = {"f": 0, "t": 0}

        def nextF():
            i = rot["f"]; rot["f"] = (i + 1) % 6
            return pF[i], f"pf{i}"

        def nextT():
            i = rot["t"]; rot["t"] = (i + 1) % 2
            return pT[i], f"pt{i}"

        d_xs = [P.dma_sem("d_x0"), P.dma_sem("d_x1")]
        d_outs = [P.dma_sem("d_out0"), P.dma_sem("d_out1")]
        d_slot = [P.dma_sem(f"d_slot{i}") for i in range(NSLOT)]
        d_sc = [P.dma_sem(f"d_sc{i}") for i in range(NCK)]
        d_p = [P.dma_sem(f"d_p{i}") for i in range(6)]

        def load_gain(gd):
            P.op("sp", lambda e: e.dma_start(out=g_bc[:], in_=gd.to_broadcast([128, D])), writes=["g_bc"], dsem=d_p[0])
        P.op("sp", lambda e: e.dma_start(out=mhg_bc[:], in_=mhg_d.to_broadcast([128, MW])), writes=["mhg_bc"], dsem=d_p[3])
        P.op("sp", lambda e: e.dma_start(out=gbias[:], in_=gb_d.to_broadcast([128, 128])), writes=["gbias"], dsem=d_p[4])
        P.op("dve", lambda e: e.memset(prm_r[:], 0.0), writes=["prm_r"])
        P.op("sp", lambda e: e.dma_start(out=prm_r[0:NPR, :], in_=prm_d), writes=["prm_r"], dsem=d_p[5])
        P.op("pool", lambda e: e.memset(ident_f[:], 1.0), writes=["ident_f"])
        P.op("pool", lambda e: e.affine_select(out=ident_f[:], in_=ident_f[:], pattern=[[-1, 128]], compare_op=ALU.is_equal,
                                               fill=0.0, base=0, channel_multiplier=1), reads=["ident_f"], writes=["ident_f"])
        P.op("pool", lambda e: e.memset(tri[:], 1.0), writes=["tri"])
        P.op("pool", lambda e: e.affine_select(out=tri[:], in_=tri[:], pattern=[[1, 128]], compare_op=ALU.is_ge,
                                               fill=0.0, base=0, channel_multiplier=-1), reads=["tri"], writes=["tri"])
        cast_state = {"issued": 0}
        CAST_AHEAD = 3

        def issue_casts(upto):
            while cast_state["issued"] <= min(upto, NCK - 1):
                i = cast_state["issued"]
                nm, src, a, b = chunks[i]
                P.op("pool", (lambda i, src, a: lambda e: e.dma_start(
                    out=scr[i].rearrange("p (a b) -> p a b", a=a), in_=src))(i, src, a),
                    writes=[f"sc{i}"], dsem=d_sc[i])
                cast_state["issued"] += 1
        P.op("dve", lambda e: e.tensor_copy(out=ident_b[:], in_=ident_f[:]), reads=["ident_f"], writes=["ident_b"])
        P.op("dve", lambda e: e.memset(epsb[:], EPS), writes=["epsb"])
        for g in range(G):
            w = 2 ** (g + 1)
            for t in range(15):
                P.op("dve", (lambda g, t, w: lambda e: e.memset(rc15[:, g, t:t + 1], 1.0 / min(t + 1, w)))(g, t, w),
                     reads=["rc15"] if False else [], writes=["rc15"])
        P.op("dve", lambda e: e.memset(vaug[:, :, :, DH:DH + 1], 1.0), writes=["vaug"])
        bk, bkk = nextF()
        P.op("pe", lambda e: e.matmul(bk[:, 0:128], lhsT=prm_r[:], rhs=ident_f[:], start=True, stop=True),
             reads=["prm_r", "ident_f"], writes=[bkk])
        P.op("dve", lambda e: e.tensor_copy(out=pcol[:], in_=bk[:, 0:128]), reads=[bkk], writes=["pcol"])
        R_CONV, R_BG, R_PS = 0, 4 * 2 * MT, 4 * 2 * MT + 2 * DT

        def conv_col(j, ft):
            r = R_CONV + j * 2 * MT + ft
            return pcol[:, r:r + 1]

        stream = {"issued": 0}

        def issue_load(gi):
            issue_casts(gi + CAST_AHEAD)
            ci = gi % NCK
            assert f"sc{ci}" in P.last_w, f"load of chunk {ci} built before its cast"
            s = gi % NSLOT
            nm, src, a, b = chunks[ci]
            P.op("sp", (lambda s, ci, a, b: lambda e: e.dma_start(out=slots[s][:, 0:a * b], in_=scr[ci]))(s, ci, a, b),
                 reads=[f"sc{ci}"], writes=[f"slot{s}"], dsem=d_slot[s])

        def wgroup(j, names):
            gi0 = j * NCK + cidx[names[0]]
            assert len(names) <= NSLOT and all(cidx[n] == cidx[names[0]] + i for i, n in enumerate(names))
            lim = min(gi0 + NSLOT - 1, NT * NCK - 1)
            while stream["issued"] <= lim:
                issue_load(stream["issued"])
                stream["issued"] += 1
            res = []
            for i, name in enumerate(names):
                nm, src, a, b = chunks[cidx[name]]
                s = (gi0 + i) % NSLOT
                res.append((slots[s][:, 0:a * b].rearrange("p (a b) -> p a b", a=a), f"slot{s}"))
            return res

        def wchunk(j, name):
            return wgroup(j, [name])[0]

        def mm_group(out_ap, pairs, reads, writes, first=True, last=True):
            def fn(e):
                n = len(pairs)
                ins = None
                for i, (l, r) in enumerate(pairs):
                    ins = e.matmul(out_ap, lhsT=l, rhs=r, start=(first and i == 0), stop=(last and i == n - 1))
                return ins
            P.op("pe", fn, reads=reads, writes=writes)

        xnk = [f"xn{cc}" for cc in range(NCH)]

        def xkeys(p):
            return [f"xt{p}_{cc}" for cc in range(NCH)]

        def rmsnorm(gain_d, src, skeys, dst, dkeys):
            load_gain(gain_d)
            P.op("dve", lambda e: e.memset(ssq[:], 0.0), writes=["ssq"])
            for cc in range(NCH):
                P.op("act", (lambda cc, src: lambda e: e.activation(out=sqj[:], in_=src[:, cc, :], func=AF.Square,
                                                                    accum_out=ssq[:, cc:cc + 1]))(cc, src),
                     reads=[skeys[cc], "ssq", "sqj"], writes=["sqj", "ssq"])
            P.op("act", lambda e: e.activation(out=rstd[:], in_=ssq[:], func=AF.Sqrt, bias=epsb[:, 0:1], scale=1.0 / D),
                 reads=["ssq", "epsb"], writes=["rstd"])
            P.op("dve", lambda e: e.reciprocal(out=rstd[:], in_=rstd[:]), reads=["rstd"], writes=["rstd"])
            for cc in range(NCH):
                P.op("dve", (lambda cc, src: lambda e: e.scalar_tensor_tensor(out=dst(cc), in0=src[:, cc, :], scalar=rstd[:, cc:cc + 1],
                                                                              in1=g_bc[:], op0=ALU.mult, op1=ALU.mult))(cc, src),
                     reads=[skeys[cc], "rstd", "g_bc", dkeys[cc]], writes=[dkeys[cc]])

        def to_feat(src_w, dstT, dkey, ntile):
            for cc in range(NCH):
                bk, bkk = nextT()

                def fn(e, cc=cc, bk=bk):
                    ins = None
                    for i in range(ntile):
                        ins = e.transpose(out=bk[:, i, :], in_=xn[:, cc, i * 128:(i + 1) * 128], identity=ident_b[:])
                    return ins
                P.op("pe", fn, reads=[xnk[cc], "ident_b"], writes=[bkk])
                P.op("act", (lambda cc, bk: lambda e: e.copy(out=dstT[:, 0:ntile, cc * 128:(cc + 1) * 128], in_=bk[:, 0:ntile, :]))(cc, bk),
                     reads=[bkk], writes=[dkey])

        store_toks = [None, None]
        pending_final = []
        for j in range(NT):
            first = (j % TPS == 0)
            last_tile = (j % TPS == TPS - 1)
            r0 = j * T
            xt = xts[j % 2]
            xk = xkeys(j % 2)

            def load_x(jj):
                p_ = jj % 2
                P.op("sp", (lambda rr, p_: lambda e: e.dma_start(out=xts[p_][:], in_=x_d[rr:rr + T, :].rearrange("(c p) d -> p c d", p=128)))(jj * T, p_),
                     writes=xkeys(p_), dsem=d_xs[p_])

            def norm1(jj):
                rmsnorm(g1_d, xts[jj % 2], xkeys(jj % 2), lambda cc: xn[:, cc, 0:D], xnk)

            if j == 0:
                load_x(0)
                norm1(0)
                to_feat(D, hT, "hT", DT)

            W, wk = wchunk(j, "gp")
            for cc in range(NCH):
                bk, bkk = nextF()
                mm_group(bk[:, 0:2 * H], [(hT[:, k, cc * 128:(cc + 1) * 128], W[:, k, 0:2 * H]) for k in range(DT)],
                         reads=["hT", wk], writes=[bkk])
                P.op("dve", (lambda cc, bk: lambda e: e.tensor_tensor(out=gates[:, cc, :], in0=bk[:, 0:2 * H], in1=gbias[:, 0:2 * H], op=ALU.add))(cc, bk),
                     reads=[bkk, "gbias"], writes=["gates"])
            for g in range(G):
                bk, bkk = nextF()
                mm_group(bk[:, 0:T], [(W[:, k, 2 * H + g * 128:2 * H + (g + 1) * 128], hT[:, k, :]) for k in range(DT)],
                         reads=["hT", wk], writes=[bkk])
                P.op("act", (lambda g, bk: lambda e: e.copy(out=praw[:, g, 15:15 + T], in_=bk[:, 0:T]))(g, bk),
                     reads=[bkk], writes=[f"praw{g}"])
            for g in range(G):
                w = 2 ** (g + 1)
                pk = f"praw{g}"
                if first:
                    P.op("pool", (lambda g: lambda e: e.memset(praw[:, g, 0:15], 0.0))(g), reads=[pk], writes=[pk])
                else:
                    P.op("pool", (lambda g: lambda e: e.tensor_copy(out=praw[:, g, 0:15], in_=phalo[:, g, :]))(g),
                         reads=[pk, f"phalo{g}"], writes=[pk])
                src, srck = praw[:, g, :], pk
                L = 15 + T
                sh = 1
                lvl = 0
                while sh < w:
                    dstb, dstk = ptmp[lvl % 2], f"ptmp{lvl % 2}"
                    lo = 2 * sh - 1
                    P.op("pool", (lambda src, dstb, lo, sh: lambda e: e.tensor_tensor(out=dstb[:, lo:L], in0=src[:, lo:L], in1=src[:, lo - sh:L - sh], op=ALU.add))(src, dstb, lo, sh),
                         reads=[srck, dstk], writes=[dstk])
                    src, srck = dstb[:, :], dstk
                    sh *= 2
                    lvl += 1
                tb, tbk = ptmp[lvl % 2], f"ptmp{lvl % 2}"
                P.op("dve", (lambda src, g, w: lambda e: e.scalar_tensor_tensor(out=pooled[:, g, :], in0=src[:, 15:15 + T], scalar=1.0 / w,
                                                                                in1=praw[:, g, 15:15 + T], op0=ALU.mult, op1=ALU.subtract))(src, g, w),
                     reads=[srck, pk, "pooled"], writes=["pooled"])
                if first:
                    P.op("pool", (lambda src, g, tb: lambda e: e.tensor_tensor(out=tb[:, 0:15], in0=src[:, 15:30], in1=rc15[:, g, :], op=ALU.mult))(src, g, tb),
                         reads=[srck, "rc15", tbk], writes=[tbk])
                    P.op("pool", (lambda g, tb: lambda e: e.tensor_tensor(out=pooled[:, g, 0:15], in0=tb[:, 0:15], in1=praw[:, g, 15:30], op=ALU.subtract))(g, tb),
                         reads=[tbk, pk, "pooled"], writes=["pooled"])
                if not last_tile:
                    P.op("pool", (lambda g: lambda e: e.tensor_copy(out=phalo[:, g, :], in_=praw[:, g, T:T + 15]))(g),
                         reads=[pk], writes=[f"phalo{g}"])
            P.op("act", lambda e: e.activation(out=nlf[:], in_=gates[:, :, H:2 * H], func=AF.Exp, scale=-1.0), reads=["gates"], writes=["nlf"])
            P.op("act", lambda e: e.activation(out=nlf[:], in_=nlf[:], func=AF.Ln, bias=1.0, scale=1.0), reads=["nlf"], writes=["nlf"])
            bk, bkk = nextF()

            def fn_na(e, bk=bk):
                ins = None
                for cc in range(NCH):
                    ins = e.matmul(bk[:, cc * H:(cc + 1) * H], lhsT=tri[:], rhs=nlf[:, cc, :], start=True, stop=True)
                return ins
            P.op("pe", fn_na, reads=["nlf", "tri"], writes=[bkk])
            P.op("dve", (lambda bk: lambda e: e.tensor_tensor(out=ebl[:], in0=bk[:, 0:NCH * H].rearrange("p (c h) -> p c h", c=NCH),
                                                              in1=gates[:, :, 0:H], op=ALU.add))(bk), reads=[bkk, "gates"], writes=["ebl"])
            P.op("act", lambda e: e.activation(out=ebl[:], in_=ebl[:], func=AF.Exp), reads=["ebl"], writes=["ebl"])
            for cc in range(NCH):
                bk, bkk = nextF()

                def fn(e, cc=cc, bk=bk):
                    ins = None
                    for h in range(H):
                        ins = e.matmul(bk[:, h * 128:(h + 1) * 128], lhsT=nlf[:, cc, h:h + 1].to_broadcast([128, 128]), rhs=tri[:],
                                       start=True, stop=True)
                    return ins
                P.op("pe", fn, reads=["nlf", "tri"], writes=[bkk])
                P.op("act", (lambda cc, bk: lambda e: e.activation(out=AB[:, cc, :, :], in_=bk[:, 0:H * 128].rearrange("p (h l) -> p h l", h=H),
                                                                   func=AF.Exp, scale=-1.0))(cc, bk), reads=[bkk], writes=[f"AB{cc}"])
            ABk = [f"AB{cc}" for cc in range(NCH)]
            P.op("dve", lambda e: e.tensor_tensor(out=ebd[:], in0=ebl[:], in1=AB[:, :, :, 127], op=ALU.mult), reads=["ebl"] + ABk, writes=["ebd"])

            def qk_ftile(W, wk, ii, ft):
                isq = ft < MT
                hh = (ft % MT) // KT
                bk, bkk = nextF()
                mm_group(bk[:, 0:T], [(W[:, k, ii * 128:(ii + 1) * 128], hT[:, k, :]) for k in range(DT)],
                         reads=["hT", wk], writes=[bkk])
                b = ft % 2
                cb, ca, cs = cbuf[b], cacc[b], csil[b]
                cbk, cak, csk = f"cbuf{b}", f"cacc{b}", "csil0"
                P.op("act", (lambda bk, cb: lambda e: e.copy(out=cb[:, 3:3 + T], in_=bk[:, 0:T]))(bk, cb), reads=[bkk], writes=[cbk])
                if first:
                    P.op("dve", (lambda cb: lambda e: e.memset(cb[:, 0:3], 0.0))(cb), reads=[cbk], writes=[cbk])
                else:
                    P.op("act", (lambda cb, ft: lambda e: e.copy(out=cb[:, 0:3], in_=chalo[:, ft, :]))(cb, ft),
                         reads=[cbk, f"chalo{ft}"], writes=[cbk])
                P.op("dve", (lambda cb, ca, ft: lambda e: e.tensor_scalar_mul(out=ca[:], in0=cb[:, 0:T], scalar1=conv_col(0, ft)))(cb, ca, ft),
                     reads=[cbk, "pcol"], writes=[cak])
                for jj in range(1, 4):
                    P.op("dve", (lambda cb, ca, ft, jj: lambda e: e.scalar_tensor_tensor(
                        out=ca[:], in0=cb[:, jj:jj + T], scalar=conv_col(jj, ft), in1=ca[:], op0=ALU.mult, op1=ALU.add))(cb, ca, ft, jj),
                        reads=[cbk, "pcol", cak], writes=[cak])
                if not last_tile:
                    P.op("act", (lambda cb, ft: lambda e: e.copy(out=chalo[:, ft, :], in_=cb[:, T:T + 3]))(cb, ft),
                         reads=[cbk], writes=[f"chalo{ft}"])

                def tail():
                    if isq:
                        P.op("act", (lambda ca, cs: lambda e: e.activation(out=cs[:], in_=ca[:], func=AF.Silu))(ca, cs), reads=[cak, csk], writes=[csk])
                        P.op("dve", (lambda cs, ft, hh: lambda e: e.scalar_tensor_tensor(
                            out=qT[:, ft, :].rearrange("p (c l) -> p c l", c=NCH), in0=cs[:].rearrange("p (c l) -> p c l", c=NCH),
                            scalar=float(DH) ** -0.5, in1=AB[:, :, hh, :], op0=ALU.mult, op1=ALU.mult))(cs, ft, hh),
                            reads=[csk] + ABk + ["qT"], writes=["qT"])
                    else:
                        P.op("act", (lambda ca, ft: lambda e: e.activation(out=kT[:, ft - MT, :], in_=ca[:], func=AF.Silu))(ca, ft),
                             reads=[cak, "kT"], writes=["kT"])
                return tail

            def vo_unit(kind, vi, W, wk, cc):
                bk, bkk = nextF()
                mm_group(bk[:, 0:512], [(hT[:, k, cc * 128:(cc + 1) * 128], W[:, k, :]) for k in range(DT)],
                         reads=["hT", wk], writes=[bkk])
                if kind == "v":
                    nh = 512 // DH
                    P.op("act", (lambda cc, bk, vi, nh: lambda e: e.copy(out=vaug[:, cc, vi * nh:(vi + 1) * nh, 0:DH],
                                                                         in_=bk[:, 0:512].rearrange("p (a b) -> p a b", a=nh)))(cc, bk, vi, nh),
                         reads=[bkk, "vaug"], writes=["vaug"])
                else:
                    P.op("act", (lambda cc, bk, vi: lambda e: e.activation(out=osig[:, cc, vi * 512:(vi + 1) * 512], in_=bk[:, 0:512], func=AF.Sigmoid))(cc, bk, vi),
                         reads=[bkk, "osig"], writes=["osig"])

            def ktok_transposes():
                for cc in range(NCH):
                    bk, bkk = nextT()

                    def fn(e, cc=cc, bk=bk):
                        ins = None
                        for i in range(MT):
                            ins = e.transpose(out=bk[:, i, :], in_=kT[:, i, cc * 128:(cc + 1) * 128], identity=ident_b[:])
                        return ins
                    P.op("pe", fn, reads=["kT", "ident_b"], writes=[bkk])
                    for h in range(H):
                        P.op("dve", (lambda cc, bk, h: lambda e: e.tensor_scalar_mul(
                            out=ktok[:, cc, h, :].rearrange("p (a b) -> p a b", a=KT), in0=bk[:, h * KT:(h + 1) * KT, :], scalar1=ebd[:, cc, h:h + 1]))(cc, bk, h),
                            reads=[bkk, "ebd", "ktok"], writes=["ktok"])

            for p_ in range(NQK):
                i = qk_order[p_]
                kind, vi = vo_list[p_]
                (W, wk), (Wv, wvk) = wgroup(j, [f"qk{i}", f"{kind}{vi}"])
                prev_tail = None
                for ii in range(4):
                    tail = qk_ftile(W, wk, ii, i * 4 + ii)
                    if prev_tail is not None:
                        prev_tail()
                    prev_tail = tail
                for cc in range(NCH):
                    vo_unit(kind, vi, Wv, wvk, cc)
                prev_tail()
                if p_ == NQK // 2 - 1:
                    ktok_transposes()

            def gch(cc):
                return (j % TPS) * NCH + cc

            def emit_ST(cc):
                bS, bSk = nextF()

                def fnS(e, cc=cc, bS=bS):
                    ins = None
                    for h in range(H):
                        for kt in range(KT):
                            ins = e.matmul(bS[:, h * 128:(h + 1) * 128], lhsT=kT[:, h * KT + kt, cc * 128:(cc + 1) * 128],
                                           rhs=qT[:, h * KT + kt, cc * 128:(cc + 1) * 128], start=(kt == 0), stop=(kt == KT - 1))
                    return ins
                P.op("pe", fnS, reads=["kT", "qT"], writes=[bSk])
                pt, ptk = PT[cc % 2], f"PT{cc % 2}"
                for h in range(H):
                    P.op("dve", (lambda cc, bS, h, pt: lambda e: e.scalar_tensor_tensor(
                        out=pt[:, h, :], in0=bS[:, h * 128:(h + 1) * 128], scalar=ebl[:, cc, h:h + 1], in1=tri[:], op0=ALU.mult, op1=ALU.mult))(cc, bS, h, pt),
                        reads=[bSk, "ebl", "tri", ptk], writes=[ptk])

            def emit_U(cc):
                g = gch(cc)
                nb = Cbf[(g + 1) % 2]
                for h in range(H):
                    for kt in range(KT):
                        bU, bUk = nextF()
                        mm_group(bU[:, 0:DH + 1], [(ktok[:, cc, h, kt * 128:(kt + 1) * 128], vaug[:, cc, h, :])],
                                 reads=["ktok", "vaug"], writes=[bUk])
                        if g > 0:
                            P.op("dve", (lambda bU, h, kt, cc: lambda e: e.scalar_tensor_tensor(
                                out=Cst[:, h, kt, :], in0=Cst[:, h, kt, :], scalar=AB[:, cc, h, 127:128], in1=bU[:, 0:DH + 1],
                                op0=ALU.mult, op1=ALU.add))(bU, h, kt, cc), reads=[bUk, f"AB{cc}", f"Cst{h}"], writes=[f"Cst{h}"])
                        else:
                            P.op("dve", (lambda bU, h, kt: lambda e: e.tensor_copy(out=Cst[:, h, kt, :], in_=bU[:, 0:DH + 1]))(bU, h, kt),
                                 reads=[bUk, f"Cst{h}"], writes=[f"Cst{h}"])
                    P.op("pool", (lambda h, nb: lambda e: e.tensor_copy(out=nb[:, h, :, :], in_=Cst[:, h, :, :]))(h, nb),
                         reads=[f"Cst{h}", f"Cbf{(g + 1) % 2}_{h}"], writes=[f"Cbf{(g + 1) % 2}_{h}"])

            def head_front(cc, h):
                g = gch(cc)
                pt, ptk = PT[cc % 2], f"PT{cc % 2}"
                cb = Cbf[g % 2]
                bN, bNk = nextF()
                pairs = [(pt[:, h, :], vaug[:, cc, h, :])]
                if g > 0:
                    pairs += [(qT[:, h * KT + kt, cc * 128:(cc + 1) * 128], cb[:, h, kt, :]) for kt in range(KT)]
                mm_group(bN[:, 0:DH + 1], pairs, reads=[ptk, "vaug", "qT", f"Cbf{g % 2}_{h}"], writes=[bNk])
                b = h % 2
                sm, smk = hsm[b], f"hsm{b}"
                hj, hjk = hjunk[0], "hjunk0"
                P.op("dve", (lambda bN, sm: lambda e: e.tensor_copy(out=sm[:, 4:5], in_=bN[:, DH:DH + 1]))(bN, sm), reads=[bNk, smk], writes=[smk])
                P.op("dve", (lambda sm: lambda e: e.scalar_tensor_tensor(out=sm[:, 0:1], in0=sm[:, 4:5], scalar=-1.0, in1=sm[:, 4:5],
                                                                         op0=ALU.mult, op1=ALU.max))(sm), reads=[smk], writes=[smk])
                P.op("dve", (lambda sm: lambda e: e.tensor_scalar_max(out=sm[:, 0:1], in0=sm[:, 0:1], scalar1=1.0))(sm), reads=[smk], writes=[smk])
                P.op("dve", (lambda sm: lambda e: e.reciprocal(out=sm[:, 0:1], in_=sm[:, 0:1]))(sm), reads=[smk], writes=[smk])
                P.op("dve", (lambda sm: lambda e: e.memset(sm[:, 1:2], 0.0))(sm), reads=[smk], writes=[smk])
                P.op("act", (lambda bN, sm, hj: lambda e: e.activation(out=hj[:], in_=bN[:, 0:DH], func=AF.Square, scale=sm[:, 0:1],
                                                                       accum_out=sm[:, 1:2]))(bN, sm, hj), reads=[bNk, smk, hjk], writes=[hjk, smk])
                P.op("act", (lambda sm: lambda e: e.activation(out=sm[:, 2:3], in_=sm[:, 1:2], func=AF.Sqrt, bias=epsb[:, 0:1], scale=1.0 / DH))(sm),
                     reads=[smk, "epsb"], writes=[smk])
                return bN, bNk

            def head_tail(cc, h, bN, bNk):
                b = h % 2
                sm, smk = hsm[b], f"hsm{b}"
                ht, htk = htmp[b], f"htmp{b}"
                P.op("dve", (lambda sm: lambda e: e.reciprocal(out=sm[:, 2:3], in_=sm[:, 2:3]))(sm), reads=[smk], writes=[smk])
                P.op("dve", (lambda sm: lambda e: e.tensor_tensor(out=sm[:, 3:4], in0=sm[:, 0:1], in1=sm[:, 2:3], op=ALU.mult))(sm), reads=[smk], writes=[smk])
                P.op("dve", (lambda bN, sm, ht, h: lambda e: e.scalar_tensor_tensor(out=ht[:], in0=bN[:, 0:DH], scalar=sm[:, 3:4],
                                                                                   in1=mhg_bc[:, h * DH:(h + 1) * DH], op0=ALU.mult, op1=ALU.mult))(bN, sm, ht, h),
                     reads=[bNk, smk, "mhg_bc", htk], writes=[htk])
                P.op("pool", (lambda ht, cc, h: lambda e: e.tensor_tensor(out=xn[:, cc, h * DH:(h + 1) * DH], in0=ht[:], in1=osig[:, cc, h * DH:(h + 1) * DH], op=ALU.mult))(ht, cc, h),
                     reads=[htk, "osig", xnk[cc]], writes=[xnk[cc]])

            emit_ST(0)
            for cc in range(NCH):
                if gch(cc) != S // 128 - 1:
                    emit_U(cc)
                if cc + 1 < NCH:
                    emit_ST(cc + 1)
                pend = None
                for h in range(H):
                    bN, bNk = head_front(cc, h)
                    if pend is not None:
                        head_tail(cc, *pend)
                    pend = (h, bN, bNk)
                head_tail(cc, *pend)
            to_feat(MW, hmT, "hmT", MT)

            while pending_final:
                pending_final.pop(0)()
            W, wk = wchunk(j, "wpool")
            for g in range(G):
                bk, bkk = nextF()
                mm_group(bk[:, 0:T], [(W[:, g, :], pooled[:, g, :])], reads=["pooled", wk], writes=[bkk])
                P.op("act", (lambda g, bk: lambda e: e.mul(out=ysT[:, g, :], in_=bk[:, 0:T], mul=pcol[:, R_PS + g:R_PS + g + 1]))(g, bk),
                     reads=[bkk, "pcol", "ysT"], writes=["ysT"])

            for hf in range(NHD):
                (Wa, wak), (Wb, wbk), (Wga, wgak), (Wgb, wgbk) = wgroup(j, [f"wa{hf}", f"wb{hf}", f"ga{hf}", f"gb{hf}"])
                for i in range(TPC):
                    dt = hf * TPC + i
                    cs_ = slice(i * 128, (i + 1) * 128)
                    bA, bAk = nextF()
                    mm_group(bA[:, 0:T], [(Wa[:, k, cs_], hmT[:, k, :]) for k in range(MT)], reads=["hmT", wak], writes=[bAk])
                    bB, bBk = nextF()
                    mm_group(bB[:, 0:T], [(Wb[:, g, cs_], ysT[:, g, :]) for g in range(G)], reads=["ysT", wbk], writes=[bBk])
                    bGa, bGak = nextF()
                    mm_group(bGa[:, 0:T], [(Wga[:, k, cs_], hT[:, k, :]) for k in range(DT)], reads=["hT", wgak], writes=[bGak])
                    bGb, bGbk = nextF()
                    mm_group(bGb[:, 0:T], [(Wgb[:, k, cs_], hT[:, k, :]) for k in range(DT)], reads=["hT", wgbk], writes=[bGbk])
                    b = 0
                    P.op("act", (lambda bGa, b, dt: lambda e: e.activation(out=sga[b][:], in_=bGa[:, 0:T], func=AF.Sigmoid,
                                                                           bias=pcol[:, R_BG + dt:R_BG + dt + 1], scale=1.0))(bGa, b, dt),
                         reads=[bGak, "pcol", f"sga{b}"], writes=[f"sga{b}"])
                    P.op("act", (lambda bGb, b, dt: lambda e: e.activation(out=sgb[b][:], in_=bGb[:, 0:T], func=AF.Sigmoid,
                                                                           bias=pcol[:, R_BG + DT + dt:R_BG + DT + dt + 1], scale=1.0))(bGb, b, dt),
                         reads=[bGbk, "pcol", f"sgb{b}"], writes=[f"sgb{b}"])
                    P.op("dve", (lambda bA, b: lambda e: e.tensor_tensor(out=sga[b][:], in0=sga[b][:], in1=bA[:, 0:T], op=ALU.mult))(bA, b),
                         reads=[bAk, f"sga{b}"], writes=[f"sga{b}"])
                    P.op("dve", (lambda bB, b: lambda e: e.tensor_tensor(out=sgb[b][:], in0=sgb[b][:], in1=bB[:, 0:T], op=ALU.mult))(bB, b),
                         reads=[bBk, f"sgb{b}"], writes=[f"sgb{b}"])
                    P.op("pool", (lambda b, dt: lambda e: e.tensor_tensor(out=mixT[:, dt, :], in0=sga[b][:], in1=sgb[b][:], op=ALU.add))(b, dt),
                         reads=[f"sga{b}", f"sgb{b}", MIXK], writes=[MIXK])
            for hf in range(NHD):
                W, wk = wchunk(j, f"wo{hf}")
                for cc in range(NCH):
                    bk, bkk = nextF()
                    mm_group(bk[:, 0:CWD], [(mixT[:, k, cc * 128:(cc + 1) * 128], W[:, k, :]) for k in range(DT)],
                             reads=[MIXK, wk], writes=[bkk])
                    P.op("dve", (lambda cc, bk, hf, xt: lambda e: e.tensor_tensor(out=xt[:, cc, hf * CWD:(hf + 1) * CWD], in0=xt[:, cc, hf * CWD:(hf + 1) * CWD],
                                                                                   in1=bk[:, 0:CWD], op=ALU.add))(cc, bk, hf, xt),
                         reads=[bkk, xk[cc]], writes=[xk[cc]])

            rmsnorm(g2_d, xt, xk, lambda cc: xn[:, cc, 0:D], xnk)
            to_feat(D, hT, "hT", DT)
            if j + 1 < NT:
                load_x(j + 1)
            for i, (f0, n) in enumerate(fch):
                (Wg, wgk), (Wu, wuk) = wgroup(j, [f"wg{i}", f"wu{i}"])
                for ii in range(n):
                    f = f0 + ii
                    cs_ = slice(ii * 128, (ii + 1) * 128)
                    bG, bGk = nextF()
                    mm_group(bG[:, 0:T], [(Wg[:, k, cs_], hT[:, k, :]) for k in range(DT)], reads=["hT", wgk], writes=[bGk])
                    bU, bUk = nextF()
                    mm_group(bU[:, 0:T], [(Wu[:, k, cs_], hT[:, k, :]) for k in range(DT)], reads=["hT", wuk], writes=[bUk])
                    b = 0
                    P.op("act", (lambda bG, b: lambda e: e.activation(out=fsg[b][:], in_=bG[:, 0:T], func=AF.Silu))(bG, b),
                         reads=[bGk, f"fsg{b}"], writes=[f"fsg{b}"])
                    av, avk = actT(f)
                    P.op("dve", (lambda bU, b, av: lambda e: e.tensor_tensor(out=av, in0=fsg[b][:], in1=bU[:, 0:T], op=ALU.mult))(bU, b, av),
                         reads=[bUk, f"fsg{b}", avk], writes=[avk])
            if j + 1 < NT:
                norm1(j + 1)
            for hf in range(NHD):
                if j + 1 < NT and hf == NHD - 1:
                    to_feat(D, hT, "hT", DT)
                banks = [nextF() for _ in range(NCH)]
                for i, (f0, n) in enumerate(dch):
                    W, wk = wchunk(j, f"wd{hf}_{i}")
                    for cc in range(NCH):
                        bk, bkk = banks[cc]
                        pairs, rk = [], set()
                        for ii in range(n):
                            av, avk = actT(f0 + ii)
                            pairs.append((av[:, cc * 128:(cc + 1) * 128], W[:, ii, :]))
                            rk.add(avk)
                        mm_group(bk[:, 0:CWD], pairs, reads=sorted(rk) + [wk], writes=[bkk], first=(i == 0), last=(i == len(dch) - 1))
                for cc in range(NCH):
                    bk, bkk = banks[cc]
                    P.op("dve", (lambda cc, bk, hf, xt: lambda e: e.tensor_tensor(out=xt[:, cc, hf * CWD:(hf + 1) * CWD], in0=xt[:, cc, hf * CWD:(hf + 1) * CWD],
                                                                                   in1=bk[:, 0:CWD], op=ALU.add))(cc, bk, hf, xt),
                         reads=[bkk, xk[cc]], writes=[xk[cc]])
            def final_and_store(j=j, xt=xt, xk=xk, r0=r0):
                rmsnorm(gf_d, xt, xk, (lambda xt: lambda cc: xt[:, cc, :])(xt), xk)
                tok = P.op("sp", (lambda r0, xt: lambda e: e.dma_start(out=out_d[r0:r0 + T, :].rearrange("(c p) d -> p c d", p=128), in_=xt[:]))(r0, xt),
                           reads=xk, writes=[f"out_d{j % 2}"], dsem=d_outs[j % 2], store=True)
                store_toks[j % 2] = tok
            if j == NT - 1:
                final_and_store()
            else:
                pending_final.append(final_and_store)
        P.final_wait("sp", [t_ for t_ in store_toks if t_ is not None])
        P.emit()
    return nc


def host_inputs(cfg, x, norm1_g, w_in, conv_qk, b_igate, b_fgate, mh_norm_g, w_pool, pool_scale,
                w_branch_a, w_branch_b, b_gate, w_out, norm2_g, w_ffn_gate, w_ffn_up, w_ffn_down, final_norm_g):
    c = derived(cfg)
    f = lambda a: np.ascontiguousarray(np.asarray(a, dtype=np.float32))
    H = c["H"]
    prm = np.concatenate([f(conv_qk)[0].reshape(-1, 128), f(b_gate)[0].reshape(-1, 128), f(pool_scale)[0].reshape(-1, 128)], axis=0)
    gb = np.zeros((1, 128), np.float32)
    gb[0, 0:H] = f(b_igate)[0]
    gb[0, H:2 * H] = f(b_fgate)[0]
    shared = {
        "w_in": f(w_in)[0], "w_pool": f(w_pool)[0], "w_a": f(w_branch_a)[0], "w_b": f(w_branch_b)[0], "w_out": f(w_out)[0],
        "wg": f(w_ffn_gate)[0], "wu": f(w_ffn_up)[0], "wd": f(w_ffn_down)[0],
        "g1": f(norm1_g)[0:1], "g2": f(norm2_g)[0:1], "gf": f(final_norm_g).reshape(1, -1), "mhg": f(mh_norm_g)[0:1],
        "prm": f(prm), "gb": gb,
    }
    return shared


def kernel(**inputs):
    cfg = FULL
    x = np.asarray(inputs["x"], dtype=np.float32)
    B, S, D = x.shape
    n = 8
    per = B // n
    shared = host_inputs(cfg, **inputs)
    nc = build(cfg)
    in_maps = []
    for i in range(n):
        m = dict(shared)
        m["x"] = np.ascontiguousarray(x[i * per:(i + 1) * per].reshape(per * S, D))
        in_maps.append(m)
    res = run_bass_kernel_spmd(nc, in_maps, core_ids=list(range(n)))
    out = np.concatenate([np.asarray(r["out"]).reshape(per, S, D) for r in res.results], axis=0)
    return out.astype(np.float32)
```

```python
import numpy as np
from contextlib import ExitStack
import concourse.bass as bass
import concourse.mybir as mybir
from concourse.bass_utils import run_bass_kernel_spmd

F32 = mybir.dt.float32
BF16 = mybir.dt.bfloat16
AF = mybir.ActivationFunctionType
ALU = mybir.AluOpType

FULL = dict(D=1024, H=4, DH=256, S=2048, NSEQ=2, DFF=2816, T=512)
EPS = 1e-6
NSLOT = 4


class Prog:
    ENGS = ("pe", "act", "dve", "pool", "sp")

    def __init__(self, nc, stack, self_wait=True):
        self.nc, self.stack, self.self_wait = nc, stack, self_wait
        self.ops = {e: [] for e in self.ENGS}
        self.cnt, self.sems = {}, {}
        for e in self.ENGS:
            self.sems[e] = stack.enter_context(nc.semaphore("prog_" + e))
            self.cnt[e] = 0
        self.last_w, self.readers = {}, {}
        self.waited = {e: {} for e in self.ENGS}
        self.dma_keys = {}

    def dma_sem(self, name):
        self.sems[name] = self.stack.enter_context(self.nc.semaphore(name))
        self.cnt[name] = 0
        return name

    def op(self, eng, fn, reads=(), writes=(), dsem=None, ndma=1, store=False):
        deps = []
        for b in reads:
            if b in self.last_w:
                deps.append(self.last_w[b])
        for b in writes:
            if b in self.last_w:
                deps.append(self.last_w[b])
            deps += self.readers.get(b, [])
        if dsem is None:
            self.cnt[eng] += 1
            tok = (eng, self.cnt[eng])
        else:
            keyset = tuple(sorted(map(str, reads if store else writes)))
            prev = self.dma_keys.setdefault(dsem, keyset)
            assert prev == keyset, f"DMA sem {dsem} reused for different buffers: {prev} vs {keyset}"
            self.cnt[dsem] += 16 * ndma
            tok = (dsem, self.cnt[dsem])
        waits = {}
        for (s, v) in deps:
            if s not in self.ENGS:
                assert v == self.cnt[s] or (s == dsem and v == self.cnt[s] - 16 * ndma), \
                    f"dep on stale DMA token ({s},{v}) latest={self.cnt[s]}"
            if s == eng and (eng == "pe" or not self.self_wait):
                continue
            if self.waited[eng].get(s, 0) >= v:
                continue
            if waits.get(s, 0) < v:
                waits[s] = v
        for s, v in waits.items():
            self.waited[eng][s] = v
        self.ops[eng].append((waits, fn, tok, dsem is not None))
        for b in reads:
            self.readers.setdefault(b, []).append(tok)
        for b in writes:
            self.last_w[b] = tok
            self.readers[b] = []
        return tok

    def final_wait(self, eng, toks):
        waits = {}
        for (s, v) in toks:
            waits[s] = max(waits.get(s, 0), v)
        self.ops[eng].append((waits, None, None, False))

    def emit(self):
        nc = self.nc
        with nc.Block() as block:
            def run(engname):
                def body(e):
                    for (waits, fn, tok, isdma) in self.ops[engname]:
                        for s, v in waits.items():
                            e.wait_ge(self.sems[s], v)
                        if fn is None:
                            continue
                        r = fn(e)
                        if isdma:
                            for ins in (r if isinstance(r, (list, tuple)) else [r]):
                                ins.then_inc(self.sems[tok[0]], 16)
                        else:
                            if isinstance(r, (list, tuple)):
                                r = r[-1]
                            r.then_inc(self.sems[tok[0]], 1)
                return body
            block.sync(run("sp"))
            block.tensor(run("pe"))
            block.scalar(run("act"))
            block.vector(run("dve"))
            block.gpsimd(run("pool"))


def derived(cfg):
    c = dict(cfg)
    D, H, DH, S, NSEQ, DFF, T = (cfg[k] for k in ("D", "H", "DH", "S", "NSEQ", "DFF", "T"))
    c.update(DT=D // 128, KT=DH // 128, MW=H * DH, MT=H * DH // 128, FT=DFF // 128, NCH=T // 128,
             TPS=S // T, NT=NSEQ * S // T, G=4, PW=512)
    c["NIN"] = 4 * c["MW"] + 2 * H + 512 + 2 * D
    c["CWD"] = min(512, D)
    c["NHD"] = D // c["CWD"]
    c["CWF"] = min(512, DFF)
    c["NPR"] = 4 * 2 * c["MT"] + 2 * c["DT"] + 4
    return c


def build(cfg):
    c = derived(cfg)
    D, H, DH, S, NSEQ, DFF, T = (c[k] for k in ("D", "H", "DH", "S", "NSEQ", "DFF", "T"))
    DT, KT, MW, MT, FT, NCH, TPS, NT, G, NIN, CWD, NHD, CWF, NPR = (
        c[k] for k in ("DT", "KT", "MW", "MT", "FT", "NCH", "TPS", "NT", "G", "NIN", "CWD", "NHD", "CWF", "NPR"))
    TPC = CWD // 128
    assert MW % 512 == 0 and D % CWD == 0 and DT <= 8 and MT <= 8 and NPR <= 128
    nc = bass.Bass("TRN2", target_bir_lowering=False)
    dr = lambda n, s, d, k: nc.dram_tensor(n, s, d, kind=k).ap()
    x_d = dr("x", [NSEQ * S, D], F32, "ExternalInput")
    out_d = dr("out", [NSEQ * S, D], F32, "ExternalOutput")
    w_in_d = dr("w_in", [D, NIN], F32, "ExternalInput")
    w_pool_d = dr("w_pool", [4, 128, 128], F32, "ExternalInput")
    w_a_d = dr("w_a", [MW, D], F32, "ExternalInput")
    w_b_d = dr("w_b", [512, D], F32, "ExternalInput")
    w_out_d = dr("w_out", [D, D], F32, "ExternalInput")
    wg_d = dr("wg", [D, DFF], F32, "ExternalInput")
    wu_d = dr("wu", [D, DFF], F32, "ExternalInput")
    wd_d = dr("wd", [DFF, D], F32, "ExternalInput")
    g1_d = dr("g1", [1, D], F32, "ExternalInput")
    g2_d = dr("g2", [1, D], F32, "ExternalInput")
    gf_d = dr("gf", [1, D], F32, "ExternalInput")
    mhg_d = dr("mhg", [1, MW], F32, "ExternalInput")
    prm_d = dr("prm", [NPR, 128], F32, "ExternalInput")
    gb_d = dr("gb", [1, 128], F32, "ExternalInput")

    chunks = []

    def add_chunk(name, src, a, b):
        chunks.append((name, src, a, b))

    def colv(w, c0, c1):
        return w[:, c0:c1].rearrange("(k p) c -> p k c", p=128)

    o_qk, o_v, o_o, o_g, o_gate = 0, 2 * MW, 3 * MW, 4 * MW, 4 * MW + 2 * H + 512
    GPW = 2 * H + 512
    add_chunk("gp", colv(w_in_d, o_g, o_g + GPW), DT, GPW)
    NQK = 2 * MW // 512
    NV = MW // 512
    qk_order = list(range(NQK // 2, NQK)) + list(range(NQK // 2))
    vo_list = [("v", i) for i in range(NV)] + [("o", i) for i in range(NV)]
    assert len(vo_list) == NQK
    for p_ in range(NQK):
        i = qk_order[p_]
        add_chunk(f"qk{i}", colv(w_in_d, o_qk + i * 512, o_qk + (i + 1) * 512), DT, 512)
        kind, vi = vo_list[p_]
        off = o_v if kind == "v" else o_o
        add_chunk(f"{kind}{vi}", colv(w_in_d, off + vi * 512, off + (vi + 1) * 512), DT, 512)
    add_chunk("wpool", w_pool_d.rearrange("g c d -> c g d"), 4, 128)
    for hf in range(NHD):
        add_chunk(f"wa{hf}", colv(w_a_d, hf * CWD, (hf + 1) * CWD), MT, CWD)
        add_chunk(f"wb{hf}", colv(w_b_d, hf * CWD, (hf + 1) * CWD), 4, CWD)
        add_chunk(f"ga{hf}", colv(w_in_d, o_gate + hf * CWD, o_gate + (hf + 1) * CWD), DT, CWD)
        add_chunk(f"gb{hf}", colv(w_in_d, o_gate + D + hf * CWD, o_gate + D + (hf + 1) * CWD), DT, CWD)
    for hf in range(NHD):
        add_chunk(f"wo{hf}", colv(w_out_d, hf * CWD, (hf + 1) * CWD), DT, CWD)
    fch = []
    f0 = 0
    while f0 < FT:
        n = min(CWF // 128, FT - f0)
        fch.append((f0, n))
        f0 += n
    for i, (f0, n) in enumerate(fch):
        add_chunk(f"wg{i}", colv(wg_d, f0 * 128, (f0 + n) * 128), DT, n * 128)
        add_chunk(f"wu{i}", colv(wu_d, f0 * 128, (f0 + n) * 128), DT, n * 128)
    KC2 = max(1, 4096 // CWD)
    dch = []
    f0 = 0
    while f0 < FT:
        n = min(KC2, FT - f0)
        dch.append((f0, n))
        f0 += n
    for hf in range(NHD):
        for i, (f0, n) in enumerate(dch):
            add_chunk(f"wd{hf}_{i}", wd_d[f0 * 128:(f0 + n) * 128, hf * CWD:(hf + 1) * CWD].rearrange("(k p) c -> p k c", p=128), n, CWD)
    NCK = len(chunks)
    SLOT = max(a * b for (_, _, a, b) in chunks)
    cidx = {nm: i for i, (nm, _, _, _) in enumerate(chunks)}
    scr = [dr(f"sc_{nm}", [128, a * b], BF16, "Internal") for (nm, _, a, b) in chunks]

    with ExitStack() as st:
        P = Prog(nc, st)
        sb = lambda n, s, d: st.enter_context(nc.sbuf_tensor(n, s, d))
        ps = lambda n, s, d: st.enter_context(nc.psum_tensor(n, s, d))

        ident_f = sb("ident_f", [128, 128], F32)
        ident_b = sb("ident_b", [128, 128], BF16)
        tri = sb("tri", [128, 128], F32)
        epsb = sb("epsb", [128, 1], F32)
        g_bc = sb("g_bc", [128, D], F32)
        mhg_bc = sb("mhg_bc", [128, MW], F32)
        gbias = sb("gbias", [128, 128], F32)
        prm_r = sb("prm_r", [128, 128], F32)
        pcol = sb("pcol", [128, 128], F32)
        rc15 = sb("rc15", [128, G, 15], F32)
        xts = [sb(f"xt{i}", [128, NCH, D], F32) for i in range(2)]
        sqj = sb("sqj", [128, D], BF16)
        xn = sb("xn", [128, NCH, max(D, MW)], BF16)
        hT = sb("hT", [128, DT, T], BF16)
        hmT = sb("hmT", [128, MT, T], BF16)
        qT = sb("qT", [128, MT, T], BF16)
        kT = sb("kT", [128, MT, T], BF16)
        AB = sb("AB", [128, NCH, H, 128], F32)
        vaug = sb("vaug", [128, NCH, H, DH + 1], BF16)
        ktok = sb("ktok", [128, NCH, H, DH], BF16)
        osig = sb("osig", [128, NCH, MW], BF16)
        gates = sb("gates", [128, NCH, 2 * H], F32)
        nlf = sb("nlf", [128, NCH, H], F32)
        ebl = sb("ebl", [128, NCH, H], F32)
        ebd = sb("ebd", [128, NCH, H], F32)
        ssq = sb("ssq", [128, NCH], F32)
        rstd = sb("rstd", [128, NCH], F32)
        cbuf = [sb(f"cbuf{i}", [128, T + 3], F32) for i in range(2)]
        cacc = [sb(f"cacc{i}", [128, T], F32) for i in range(2)]
        csil = [sb("csil0", [128, T], F32)] * 2
        chalo = sb("chalo", [128, 2 * MT, 3], F32)
        praw = sb("praw", [128, G, 15 + T], F32)
        ptmp = [sb(f"ptmp{i}", [128, 15 + T], F32) for i in range(2)]
        phalo = sb("phalo", [128, G, 15], F32)
        pooled = sb("pooled", [128, G, T], BF16)
        ysT = sb("ysT", [128, G, T], BF16)
        PT = [sb(f"PT{i}", [128, H, 128], BF16) for i in range(2)]
        Cst = sb("Cst", [128, H, KT, DH + 1], F32)
        Cbf = [sb(f"Cbf{i}", [128, H, KT, DH + 1], BF16) for i in range(2)]
        hsm = [sb(f"hsm{i}", [128, 8], F32) for i in range(2)]
        hjunk = [sb("hjunk0", [128, DH], BF16)] * 2
        htmp = [sb(f"htmp{i}", [128, DH], F32) for i in range(2)]
        assert DT * T <= NCH * H * DH and FT <= 3 * MT
        mixT = ktok[:].rearrange("p c h d -> p (c h d)")[:, 0:DT * T].rearrange("p (k t) -> p k t", k=DT)
        MIXK = "ktok"
        sga = [sb("sga0", [128, T], F32)] * 2
        sgb = [sb("sgb0", [128, T], F32)] * 2
        fsg = [sb("fsg0", [128, T], F32)] * 2
        slots = [sb(f"wslot{i}", [128, SLOT], BF16) for i in range(NSLOT)]

        def actT(f):
            if f < MT:
                return qT[:, f, :], "qT"
            if f < 2 * MT:
                return kT[:, f - MT, :], "kT"
            return hmT[:, f - 2 * MT, :], "hmT"

        pF = [ps(f"pf{i}", [128, 512], F32) for i in range(6)]
        pT = [ps(f"pt{i}", [128, 8, 128], BF16) for i in range(2)]
        rot = {"f": 0, "t": 0}

        def nextF():
            i = rot["f"]; rot["f"] = (i + 1) % 6
            return pF[i], f"pf{i}"

        def nextT():
            i = rot["t"]; rot["t"] = (i + 1) % 2
            return pT[i], f"pt{i}"

        d_xs = [P.dma_sem("d_x0"), P.dma_sem("d_x1")]
        d_outs = [P.dma_sem("d_out0"), P.dma_sem("d_out1")]
        d_slot = [P.dma_sem(f"d_slot{i}") for i in range(NSLOT)]
        d_sc = [P.dma_sem(f"d_sc{i}") for i in range(NCK)]
        d_p = [P.dma_sem(f"d_p{i}") for i in range(6)]

        def load_gain(gd):
            P.op("sp", lambda e: e.dma_start(out=g_bc[:], in_=gd.to_broadcast([128, D])), writes=["g_bc"], dsem=d_p[0])
        P.op("sp", lambda e: e.dma_start(out=mhg_bc[:], in_=mhg_d.to_broadcast([128, MW])), writes=["mhg_bc"], dsem=d_p[3])
        P.op("sp", lambda e: e.dma_start(out=gbias[:], in_=gb_d.to_broadcast([128, 128])), writes=["gbias"], dsem=d_p[4])
        P.op("dve", lambda e: e.memset(prm_r[:], 0.0), writes=["prm_r"])
        P.op("sp", lambda e: e.dma_start(out=prm_r[0:NPR, :], in_=prm_d), writes=["prm_r"], dsem=d_p[5])
        P.op("pool", lambda e: e.memset(ident_f[:], 1.0), writes=["ident_f"])
        P.op("pool", lambda e: e.affine_select(out=ident_f[:], in_=ident_f[:], pattern=[[-1, 128]], compare_op=ALU.is_equal,
                                               fill=0.0, base=0, channel_multiplier=1), reads=["ident_f"], writes=["ident_f"])
        P.op("pool", lambda e: e.memset(tri[:], 1.0), writes=["tri"])
        P.op("pool", lambda e: e.affine_select(out=tri[:], in_=tri[:], pattern=[[1, 128]], compare_op=ALU.is_ge,
                                               fill=0.0, base=0, channel_multiplier=-1), reads=["tri"], writes=["tri"])
        cast_state = {"issued": 0}
        CAST_AHEAD = 3

        def issue_casts(upto):
            while cast_state["issued"] <= min(upto, NCK - 1):
                i = cast_state["issued"]
                nm, src, a, b = chunks[i]
                P.op("pool", (lambda i, src, a: lambda e: e.dma_start(
                    out=scr[i].rearrange("p (a b) -> p a b", a=a), in_=src))(i, src, a),
                    writes=[f"sc{i}"], dsem=d_sc[i])
                cast_state["issued"] += 1
        P.op("dve", lambda e: e.tensor_copy(out=ident_b[:], in_=ident_f[:]), reads=["ident_f"], writes=["ident_b"])
        P.op("dve", lambda e: e.memset(epsb[:], EPS), writes=["epsb"])
        for g in range(G):
            w = 2 ** (g + 1)
            for t in range(15):
                P.op("dve", (lambda g, t, w: lambda e: e.memset(rc15[:, g, t:t + 1], 1.0 / min(t + 1, w)))(g, t, w),
                     reads=["rc15"] if False else [], writes=["rc15"])
        P.op("dve", lambda e: e.memset(vaug[:, :, :, DH:DH + 1], 1.0), writes=["vaug"])
        bk, bkk = nextF()
        P.op("pe", lambda e: e.matmul(bk[:, 0:128], lhsT=prm_r[:], rhs=ident_f[:], start=True, stop=True),
             reads=["prm_r", "ident_f"], writes=[bkk])
        P.op("dve", lambda e: e.tensor_copy(out=pcol[:], in_=bk[:, 0:128]), reads=[bkk], writes=["pcol"])
        R_CONV, R_BG, R_PS = 0, 4 * 2 * MT, 4 * 2 * MT + 2 * DT

        def conv_col(j, ft):
            r = R_CONV + j * 2 * MT + ft
            return pcol[:, r:r + 1]

        stream = {"issued": 0}

        def issue_load(gi):
            issue_casts(gi + (CAST_AHEAD if gi < 8 else 9))
            ci = gi % NCK
            assert f"sc{ci}" in P.last_w, f"load of chunk {ci} built before its cast"
            s = gi % NSLOT
            nm, src, a, b = chunks[ci]
            P.op("sp", (lambda s, ci, a, b: lambda e: e.dma_start(out=slots[s][:, 0:a * b], in_=scr[ci]))(s, ci, a, b),
                 reads=[f"sc{ci}"], writes=[f"slot{s}"], dsem=d_slot[s])

        def wgroup(j, names):
            gi0 = j * NCK + cidx[names[0]]
            assert len(names) <= NSLOT and all(cidx[n] == cidx[names[0]] + i for i, n in enumerate(names))
            lim = min(gi0 + NSLOT - 1, NT * NCK - 1)
            while stream["issued"] <= lim:
                issue_load(stream["issued"])
                stream["issued"] += 1
            res = []
            for i, name in enumerate(names):
                nm, src, a, b = chunks[cidx[name]]
                s = (gi0 + i) % NSLOT
                res.append((slots[s][:, 0:a * b].rearrange("p (a b) -> p a b", a=a), f"slot{s}"))
            return res

        def wchunk(j, name):
            return wgroup(j, [name])[0]

        def mm_group(out_ap, pairs, reads, writes, first=True, last=True):
            def fn(e):
                n = len(pairs)
                ins = None
                for i, (l, r) in enumerate(pairs):
                    ins = e.matmul(out_ap, lhsT=l, rhs=r, start=(first and i == 0), stop=(last and i == n - 1))
                return ins
            P.op("pe", fn, reads=reads, writes=writes)

        xnk = [f"xn{cc}" for cc in range(NCH)]

        def xkeys(p):
            return [f"xt{p}_{cc}" for cc in range(NCH)]

        def rmsnorm(gain_d, src, skeys, dst, dkeys):
            load_gain(gain_d)
            P.op("dve", lambda e: e.memset(ssq[:], 0.0), writes=["ssq"])
            for cc in range(NCH):
                P.op("act", (lambda cc, src: lambda e: e.activation(out=sqj[:], in_=src[:, cc, :], func=AF.Square,
                                                                    accum_out=ssq[:, cc:cc + 1]))(cc, src),
                     reads=[skeys[cc], "ssq", "sqj"], writes=["sqj", "ssq"])
            P.op("act", lambda e: e.activation(out=rstd[:], in_=ssq[:], func=AF.Sqrt, bias=epsb[:, 0:1], scale=1.0 / D),
                 reads=["ssq", "epsb"], writes=["rstd"])
            P.op("dve", lambda e: e.reciprocal(out=rstd[:], in_=rstd[:]), reads=["rstd"], writes=["rstd"])
            for cc in range(NCH):
                P.op("dve", (lambda cc, src: lambda e: e.scalar_tensor_tensor(out=dst(cc), in0=src[:, cc, :], scalar=rstd[:, cc:cc + 1],
                                                                              in1=g_bc[:], op0=ALU.mult, op1=ALU.mult))(cc, src),
                     reads=[skeys[cc], "rstd", "g_bc", dkeys[cc]], writes=[dkeys[cc]])

        def to_feat(src_w, dstT, dkey, ntile):
            for cc in range(NCH):
                bk, bkk = nextT()

                def fn(e, cc=cc, bk=bk):
                    ins = None
                    for i in range(ntile):
                        ins = e.transpose(out=bk[:, i, :], in_=xn[:, cc, i * 128:(i + 1) * 128], identity=ident_b[:])
                    return ins
                P.op("pe", fn, reads=[xnk[cc], "ident_b"], writes=[bkk])
                P.op("act", (lambda cc, bk: lambda e: e.copy(out=dstT[:, 0:ntile, cc * 128:(cc + 1) * 128], in_=bk[:, 0:ntile, :]))(cc, bk),
                     reads=[bkk], writes=[dkey])

        store_toks = [None, None]
        for j in range(NT):
            first = (j % TPS == 0)
            last_tile = (j % TPS == TPS - 1)
            r0 = j * T
            xt = xts[j % 2]
            xk = xkeys(j % 2)

            def load_x(jj):
                p_ = jj % 2
                P.op("sp", (lambda rr, p_: lambda e: e.dma_start(out=xts[p_][:], in_=x_d[rr:rr + T, :].rearrange("(c p) d -> p c d", p=128)))(jj * T, p_),
                     writes=xkeys(p_), dsem=d_xs[p_])

            def norm1(jj):
                rmsnorm(g1_d, xts[jj % 2], xkeys(jj % 2), lambda cc: xn[:, cc, 0:D], xnk)

            if j == 0:
                load_x(0)
                norm1(0)
                to_feat(D, hT, "hT", DT)

            W, wk = wchunk(j, "gp")
            for cc in range(NCH):
                bk, bkk = nextF()
                mm_group(bk[:, 0:2 * H], [(hT[:, k, cc * 128:(cc + 1) * 128], W[:, k, 0:2 * H]) for k in range(DT)],
                         reads=["hT", wk], writes=[bkk])
                P.op("dve", (lambda cc, bk: lambda e: e.tensor_tensor(out=gates[:, cc, :], in0=bk[:, 0:2 * H], in1=gbias[:, 0:2 * H], op=ALU.add))(cc, bk),
                     reads=[bkk, "gbias"], writes=["gates"])
            for g in range(G):
                bk, bkk = nextF()
                mm_group(bk[:, 0:T], [(W[:, k, 2 * H + g * 128:2 * H + (g + 1) * 128], hT[:, k, :]) for k in range(DT)],
                         reads=["hT", wk], writes=[bkk])
                P.op("act", (lambda g, bk: lambda e: e.copy(out=praw[:, g, 15:15 + T], in_=bk[:, 0:T]))(g, bk),
                     reads=[bkk], writes=[f"praw{g}"])
            for g in range(G):
                w = 2 ** (g + 1)
                pk = f"praw{g}"
                if first:
                    P.op("pool", (lambda g: lambda e: e.memset(praw[:, g, 0:15], 0.0))(g), reads=[pk], writes=[pk])
                else:
                    P.op("pool", (lambda g: lambda e: e.tensor_copy(out=praw[:, g, 0:15], in_=phalo[:, g, :]))(g),
                         reads=[pk, f"phalo{g}"], writes=[pk])
                src, srck = praw[:, g, :], pk
                L = 15 + T
                sh = 1
                lvl = 0
                while sh < w:
                    dstb, dstk = ptmp[lvl % 2], f"ptmp{lvl % 2}"
                    lo = 2 * sh - 1
                    P.op("pool", (lambda src, dstb, lo, sh: lambda e: e.tensor_tensor(out=dstb[:, lo:L], in0=src[:, lo:L], in1=src[:, lo - sh:L - sh], op=ALU.add))(src, dstb, lo, sh),
                         reads=[srck, dstk], writes=[dstk])
                    src, srck = dstb[:, :], dstk
                    sh *= 2
                    lvl += 1
                tb, tbk = ptmp[lvl % 2], f"ptmp{lvl % 2}"
                P.op("dve", (lambda src, g, w: lambda e: e.scalar_tensor_tensor(out=pooled[:, g, :], in0=src[:, 15:15 + T], scalar=1.0 / w,
                                                                                in1=praw[:, g, 15:15 + T], op0=ALU.mult, op1=ALU.subtract))(src, g, w),
                     reads=[srck, pk, "pooled"], writes=["pooled"])
                if first:
                    P.op("pool", (lambda src, g, tb: lambda e: e.tensor_tensor(out=tb[:, 0:15], in0=src[:, 15:30], in1=rc15[:, g, :], op=ALU.mult))(src, g, tb),
                         reads=[srck, "rc15", tbk], writes=[tbk])
                    P.op("pool", (lambda g, tb: lambda e: e.tensor_tensor(out=pooled[:, g, 0:15], in0=tb[:, 0:15], in1=praw[:, g, 15:30], op=ALU.subtract))(g, tb),
                         reads=[tbk, pk, "pooled"], writes=["pooled"])
                if not last_tile:
                    P.op("pool", (lambda g: lambda e: e.tensor_copy(out=phalo[:, g, :], in_=praw[:, g, T:T + 15]))(g),
                         reads=[pk], writes=[f"phalo{g}"])
            P.op("act", lambda e: e.activation(out=nlf[:], in_=gates[:, :, H:2 * H], func=AF.Exp, scale=-1.0), reads=["gates"], writes=["nlf"])
            P.op("act", lambda e: e.activation(out=nlf[:], in_=nlf[:], func=AF.Ln, bias=1.0, scale=1.0), reads=["nlf"], writes=["nlf"])
            bk, bkk = nextF()

            def fn_na(e, bk=bk):
                ins = None
                for cc in range(NCH):
                    ins = e.matmul(bk[:, cc * H:(cc + 1) * H], lhsT=tri[:], rhs=nlf[:, cc, :], start=True, stop=True)
                return ins
            P.op("pe", fn_na, reads=["nlf", "tri"], writes=[bkk])
            P.op("dve", (lambda bk: lambda e: e.tensor_tensor(out=ebl[:], in0=bk[:, 0:NCH * H].rearrange("p (c h) -> p c h", c=NCH),
                                                              in1=gates[:, :, 0:H], op=ALU.add))(bk), reads=[bkk, "gates"], writes=["ebl"])
            P.op("act", lambda e: e.activation(out=ebl[:], in_=ebl[:], func=AF.Exp), reads=["ebl"], writes=["ebl"])
            for cc in range(NCH):
                bk, bkk = nextF()

                def fn(e, cc=cc, bk=bk):
                    ins = None
                    for h in range(H):
                        ins = e.matmul(bk[:, h * 128:(h + 1) * 128], lhsT=nlf[:, cc, h:h + 1].to_broadcast([128, 128]), rhs=tri[:],
                                       start=True, stop=True)
                    return ins
                P.op("pe", fn, reads=["nlf", "tri"], writes=[bkk])
                P.op("act", (lambda cc, bk: lambda e: e.activation(out=AB[:, cc, :, :], in_=bk[:, 0:H * 128].rearrange("p (h l) -> p h l", h=H),
                                                                   func=AF.Exp, scale=-1.0))(cc, bk), reads=[bkk], writes=[f"AB{cc}"])
            ABk = [f"AB{cc}" for cc in range(NCH)]
            P.op("dve", lambda e: e.tensor_tensor(out=ebd[:], in0=ebl[:], in1=AB[:, :, :, 127], op=ALU.mult), reads=["ebl"] + ABk, writes=["ebd"])

            def qk_ftile(W, wk, ii, ft):
                isq = ft < MT
                hh = (ft % MT) // KT
                bk, bkk = nextF()
                mm_group(bk[:, 0:T], [(W[:, k, ii * 128:(ii + 1) * 128], hT[:, k, :]) for k in range(DT)],
                         reads=["hT", wk], writes=[bkk])
                b = ft % 2
                cb, ca, cs = cbuf[b], cacc[b], csil[b]
                cbk, cak, csk = f"cbuf{b}", f"cacc{b}", "csil0"
                P.op("act", (lambda bk, cb: lambda e: e.copy(out=cb[:, 3:3 + T], in_=bk[:, 0:T]))(bk, cb), reads=[bkk], writes=[cbk])
                if first:
                    P.op("dve", (lambda cb: lambda e: e.memset(cb[:, 0:3], 0.0))(cb), reads=[cbk], writes=[cbk])
                else:
                    P.op("act", (lambda cb, ft: lambda e: e.copy(out=cb[:, 0:3], in_=chalo[:, ft, :]))(cb, ft),
                         reads=[cbk, f"chalo{ft}"], writes=[cbk])
                P.op("dve", (lambda cb, ca, ft: lambda e: e.tensor_scalar_mul(out=ca[:], in0=cb[:, 0:T], scalar1=conv_col(0, ft)))(cb, ca, ft),
                     reads=[cbk, "pcol"], writes=[cak])
                for jj in range(1, 4):
                    P.op("dve", (lambda cb, ca, ft, jj: lambda e: e.scalar_tensor_tensor(
                        out=ca[:], in0=cb[:, jj:jj + T], scalar=conv_col(jj, ft), in1=ca[:], op0=ALU.mult, op1=ALU.add))(cb, ca, ft, jj),
                        reads=[cbk, "pcol", cak], writes=[cak])
                if not last_tile:
                    P.op("act", (lambda cb, ft: lambda e: e.copy(out=chalo[:, ft, :], in_=cb[:, T:T + 3]))(cb, ft),
                         reads=[cbk], writes=[f"chalo{ft}"])

                def tail():
                    if isq:
                        P.op("act", (lambda ca, cs: lambda e: e.activation(out=cs[:], in_=ca[:], func=AF.Silu))(ca, cs), reads=[cak, csk], writes=[csk])
                        P.op("dve", (lambda cs, ft, hh: lambda e: e.scalar_tensor_tensor(
                            out=qT[:, ft, :].rearrange("p (c l) -> p c l", c=NCH), in0=cs[:].rearrange("p (c l) -> p c l", c=NCH),
                            scalar=float(DH) ** -0.5, in1=AB[:, :, hh, :], op0=ALU.mult, op1=ALU.mult))(cs, ft, hh),
                            reads=[csk] + ABk + ["qT"], writes=["qT"])
                    else:
                        P.op("act", (lambda ca, ft: lambda e: e.activation(out=kT[:, ft - MT, :], in_=ca[:], func=AF.Silu))(ca, ft),
                             reads=[cak, "kT"], writes=["kT"])
                return tail

            def vo_unit(kind, vi, W, wk, cc):
                bk, bkk = nextF()
                mm_group(bk[:, 0:512], [(hT[:, k, cc * 128:(cc + 1) * 128], W[:, k, :]) for k in range(DT)],
                         reads=["hT", wk], writes=[bkk])
                if kind == "v":
                    nh = 512 // DH
                    P.op("act", (lambda cc, bk, vi, nh: lambda e: e.copy(out=vaug[:, cc, vi * nh:(vi + 1) * nh, 0:DH],
                                                                         in_=bk[:, 0:512].rearrange("p (a b) -> p a b", a=nh)))(cc, bk, vi, nh),
                         reads=[bkk, "vaug"], writes=["vaug"])
                else:
                    P.op("act", (lambda cc, bk, vi: lambda e: e.activation(out=osig[:, cc, vi * 512:(vi + 1) * 512], in_=bk[:, 0:512], func=AF.Sigmoid))(cc, bk, vi),
                         reads=[bkk, "osig"], writes=["osig"])

            def ktok_transposes():
                for cc in range(NCH):
                    bk, bkk = nextT()

                    def fn(e, cc=cc, bk=bk):
                        ins = None
                        for i in range(MT):
                            ins = e.transpose(out=bk[:, i, :], in_=kT[:, i, cc * 128:(cc + 1) * 128], identity=ident_b[:])
                        return ins
                    P.op("pe", fn, reads=["kT", "ident_b"], writes=[bkk])
                    for h in range(H):
                        P.op("dve", (lambda cc, bk, h: lambda e: e.tensor_scalar_mul(
                            out=ktok[:, cc, h, :].rearrange("p (a b) -> p a b", a=KT), in0=bk[:, h * KT:(h + 1) * KT, :], scalar1=ebd[:, cc, h:h + 1]))(cc, bk, h),
                            reads=[bkk, "ebd", "ktok"], writes=["ktok"])

            for p_ in range(NQK):
                i = qk_order[p_]
                kind, vi = vo_list[p_]
                (W, wk), (Wv, wvk) = wgroup(j, [f"qk{i}", f"{kind}{vi}"])
                prev_tail = None
                for ii in range(4):
                    tail = qk_ftile(W, wk, ii, i * 4 + ii)
                    if prev_tail is not None:
                        prev_tail()
                    prev_tail = tail
                for cc in range(NCH):
                    vo_unit(kind, vi, Wv, wvk, cc)
                prev_tail()
                if p_ == NQK // 2 - 1:
                    ktok_transposes()

            def gch(cc):
                return (j % TPS) * NCH + cc

            def emit_ST(cc):
                bS, bSk = nextF()

                def fnS(e, cc=cc, bS=bS):
                    ins = None
                    for h in range(H):
                        for kt in range(KT):
                            ins = e.matmul(bS[:, h * 128:(h + 1) * 128], lhsT=kT[:, h * KT + kt, cc * 128:(cc + 1) * 128],
                                           rhs=qT[:, h * KT + kt, cc * 128:(cc + 1) * 128], start=(kt == 0), stop=(kt == KT - 1))
                    return ins
                P.op("pe", fnS, reads=["kT", "qT"], writes=[bSk])
                pt, ptk = PT[cc % 2], f"PT{cc % 2}"
                for h in range(H):
                    P.op("dve", (lambda cc, bS, h, pt: lambda e: e.scalar_tensor_tensor(
                        out=pt[:, h, :], in0=bS[:, h * 128:(h + 1) * 128], scalar=ebl[:, cc, h:h + 1], in1=tri[:], op0=ALU.mult, op1=ALU.mult))(cc, bS, h, pt),
                        reads=[bSk, "ebl", "tri", ptk], writes=[ptk])

            def emit_U(cc):
                g = gch(cc)
                nb = Cbf[(g + 1) % 2]
                for h in range(H):
                    for kt in range(KT):
                        bU, bUk = nextF()
                        mm_group(bU[:, 0:DH + 1], [(ktok[:, cc, h, kt * 128:(kt + 1) * 128], vaug[:, cc, h, :])],
                                 reads=["ktok", "vaug"], writes=[bUk])
                        if g > 0:
                            P.op("dve", (lambda bU, h, kt, cc: lambda e: e.scalar_tensor_tensor(
                                out=Cst[:, h, kt, :], in0=Cst[:, h, kt, :], scalar=AB[:, cc, h, 127:128], in1=bU[:, 0:DH + 1],
                                op0=ALU.mult, op1=ALU.add))(bU, h, kt, cc), reads=[bUk, f"AB{cc}", f"Cst{h}"], writes=[f"Cst{h}"])
                        else:
                            P.op("dve", (lambda bU, h, kt: lambda e: e.tensor_copy(out=Cst[:, h, kt, :], in_=bU[:, 0:DH + 1]))(bU, h, kt),
                                 reads=[bUk, f"Cst{h}"], writes=[f"Cst{h}"])
                    P.op("pool", (lambda h, nb: lambda e: e.tensor_copy(out=nb[:, h, :, :], in_=Cst[:, h, :, :]))(h, nb),
                         reads=[f"Cst{h}", f"Cbf{(g + 1) % 2}_{h}"], writes=[f"Cbf{(g + 1) % 2}_{h}"])

            def head_front(cc, h):
                g = gch(cc)
                pt, ptk = PT[cc % 2], f"PT{cc % 2}"
                cb = Cbf[g % 2]
                bN, bNk = nextF()
                pairs = [(pt[:, h, :], vaug[:, cc, h, :])]
                if g > 0:
                    pairs += [(qT[:, h * KT + kt, cc * 128:(cc + 1) * 128], cb[:, h, kt, :]) for kt in range(KT)]
                mm_group(bN[:, 0:DH + 1], pairs, reads=[ptk, "vaug", "qT", f"Cbf{g % 2}_{h}"], writes=[bNk])
                b = h % 2
                sm, smk = hsm[b], f"hsm{b}"
                hj, hjk = hjunk[0], "hjunk0"
                P.op("dve", (lambda bN, sm: lambda e: e.tensor_copy(out=sm[:, 4:5], in_=bN[:, DH:DH + 1]))(bN, sm), reads=[bNk, smk], writes=[smk])
                P.op("dve", (lambda sm: lambda e: e.scalar_tensor_tensor(out=sm[:, 0:1], in0=sm[:, 4:5], scalar=-1.0, in1=sm[:, 4:5],
                                                                         op0=ALU.mult, op1=ALU.max))(sm), reads=[smk], writes=[smk])
                P.op("dve", (lambda sm: lambda e: e.tensor_scalar_max(out=sm[:, 0:1], in0=sm[:, 0:1], scalar1=1.0))(sm), reads=[smk], writes=[smk])
                P.op("dve", (lambda sm: lambda e: e.reciprocal(out=sm[:, 0:1], in_=sm[:, 0:1]))(sm), reads=[smk], writes=[smk])
                P.op("dve", (lambda sm: lambda e: e.memset(sm[:, 1:2], 0.0))(sm), reads=[smk], writes=[smk])
                P.op("act", (lambda bN, sm, hj: lambda e: e.activation(out=hj[:], in_=bN[:, 0:DH], func=AF.Square, scale=sm[:, 0:1],
                                                                       accum_out=sm[:, 1:2]))(bN, sm, hj), reads=[bNk, smk, hjk], writes=[hjk, smk])
                P.op("act", (lambda sm: lambda e: e.activation(out=sm[:, 2:3], in_=sm[:, 1:2], func=AF.Sqrt, bias=epsb[:, 0:1], scale=1.0 / DH))(sm),
                     reads=[smk, "epsb"], writes=[smk])
                return bN, bNk

            def head_tail(cc, h, bN, bNk):
                b = h % 2
                sm, smk = hsm[b], f"hsm{b}"
                ht, htk = htmp[b], f"htmp{b}"
                P.op("dve", (lambda sm: lambda e: e.reciprocal(out=sm[:, 2:3], in_=sm[:, 2:3]))(sm), reads=[smk], writes=[smk])
                P.op("dve", (lambda sm: lambda e: e.tensor_tensor(out=sm[:, 3:4], in0=sm[:, 0:1], in1=sm[:, 2:3], op=ALU.mult))(sm), reads=[smk], writes=[smk])
                P.op("dve", (lambda bN, sm, ht, h: lambda e: e.scalar_tensor_tensor(out=ht[:], in0=bN[:, 0:DH], scalar=sm[:, 3:4],
                                                                                   in1=mhg_bc[:, h * DH:(h + 1) * DH], op0=ALU.mult, op1=ALU.mult))(bN, sm, ht, h),
                     reads=[bNk, smk, "mhg_bc", htk], writes=[htk])
                P.op("pool", (lambda ht, cc, h: lambda e: e.tensor_tensor(out=xn[:, cc, h * DH:(h + 1) * DH], in0=ht[:], in1=osig[:, cc, h * DH:(h + 1) * DH], op=ALU.mult))(ht, cc, h),
                     reads=[htk, "osig", xnk[cc]], writes=[xnk[cc]])

            emit_ST(0)
            for cc in range(NCH):
                if gch(cc) != S // 128 - 1:
                    emit_U(cc)
                if cc + 1 < NCH:
                    emit_ST(cc + 1)
                pend = None
                for h in range(H):
                    bN, bNk = head_front(cc, h)
                    if pend is not None:
                        head_tail(cc, *pend)
                    pend = (h, bN, bNk)
                head_tail(cc, *pend)
            to_feat(MW, hmT, "hmT", MT)

            W, wk = wchunk(j, "wpool")
            for g in range(G):
                bk, bkk = nextF()
                mm_group(bk[:, 0:T], [(W[:, g, :], pooled[:, g, :])], reads=["pooled", wk], writes=[bkk])
                P.op("act", (lambda g, bk: lambda e: e.mul(out=ysT[:, g, :], in_=bk[:, 0:T], mul=pcol[:, R_PS + g:R_PS + g + 1]))(g, bk),
                     reads=[bkk, "pcol", "ysT"], writes=["ysT"])

            for hf in range(NHD):
                (Wa, wak), (Wb, wbk), (Wga, wgak), (Wgb, wgbk) = wgroup(j, [f"wa{hf}", f"wb{hf}", f"ga{hf}", f"gb{hf}"])
                for i in range(TPC):
                    dt = hf * TPC + i
                    cs_ = slice(i * 128, (i + 1) * 128)
                    bA, bAk = nextF()
                    mm_group(bA[:, 0:T], [(Wa[:, k, cs_], hmT[:, k, :]) for k in range(MT)], reads=["hmT", wak], writes=[bAk])
                    bB, bBk = nextF()
                    mm_group(bB[:, 0:T], [(Wb[:, g, cs_], ysT[:, g, :]) for g in range(G)], reads=["ysT", wbk], writes=[bBk])
                    bGa, bGak = nextF()
                    mm_group(bGa[:, 0:T], [(Wga[:, k, cs_], hT[:, k, :]) for k in range(DT)], reads=["hT", wgak], writes=[bGak])
                    bGb, bGbk = nextF()
                    mm_group(bGb[:, 0:T], [(Wgb[:, k, cs_], hT[:, k, :]) for k in range(DT)], reads=["hT", wgbk], writes=[bGbk])
                    b = 0
                    P.op("act", (lambda bGa, b, dt: lambda e: e.activation(out=sga[b][:], in_=bGa[:, 0:T], func=AF.Sigmoid,
                                                                           bias=pcol[:, R_BG + dt:R_BG + dt + 1], scale=1.0))(bGa, b, dt),
                         reads=[bGak, "pcol", f"sga{b}"], writes=[f"sga{b}"])
                    P.op("act", (lambda bGb, b, dt: lambda e: e.activation(out=sgb[b][:], in_=bGb[:, 0:T], func=AF.Sigmoid,
                                                                           bias=pcol[:, R_BG + DT + dt:R_BG + DT + dt + 1], scale=1.0))(bGb, b, dt),
                         reads=[bGbk, "pcol", f"sgb{b}"], writes=[f"sgb{b}"])
                    P.op("dve", (lambda bA, b: lambda e: e.tensor_tensor(out=sga[b][:], in0=sga[b][:], in1=bA[:, 0:T], op=ALU.mult))(bA, b),
                         reads=[bAk, f"sga{b}"], writes=[f"sga{b}"])
                    P.op("dve", (lambda bB, b: lambda e: e.tensor_tensor(out=sgb[b][:], in0=sgb[b][:], in1=bB[:, 0:T], op=ALU.mult))(bB, b),
                         reads=[bBk, f"sgb{b}"], writes=[f"sgb{b}"])
                    P.op("pool", (lambda b, dt: lambda e: e.tensor_tensor(out=mixT[:, dt, :], in0=sga[b][:], in1=sgb[b][:], op=ALU.add))(b, dt),
                         reads=[f"sga{b}", f"sgb{b}", MIXK], writes=[MIXK])
            for hf in range(NHD):
                W, wk = wchunk(j, f"wo{hf}")
                for cc in range(NCH):
                    bk, bkk = nextF()
                    mm_group(bk[:, 0:CWD], [(mixT[:, k, cc * 128:(cc + 1) * 128], W[:, k, :]) for k in range(DT)],
                             reads=[MIXK, wk], writes=[bkk])
                    P.op("dve", (lambda cc, bk, hf, xt: lambda e: e.tensor_tensor(out=xt[:, cc, hf * CWD:(hf + 1) * CWD], in0=xt[:, cc, hf * CWD:(hf + 1) * CWD],
                                                                                   in1=bk[:, 0:CWD], op=ALU.add))(cc, bk, hf, xt),
                         reads=[bkk, xk[cc]], writes=[xk[cc]])

            rmsnorm(g2_d, xt, xk, lambda cc: xn[:, cc, 0:D], xnk)
            to_feat(D, hT, "hT", DT)
            if j + 1 < NT:
                load_x(j + 1)
            for i, (f0, n) in enumerate(fch):
                (Wg, wgk), (Wu, wuk) = wgroup(j, [f"wg{i}", f"wu{i}"])
                for ii in range(n):
                    f = f0 + ii
                    cs_ = slice(ii * 128, (ii + 1) * 128)
                    bG, bGk = nextF()
                    mm_group(bG[:, 0:T], [(Wg[:, k, cs_], hT[:, k, :]) for k in range(DT)], reads=["hT", wgk], writes=[bGk])
                    bU, bUk = nextF()
                    mm_group(bU[:, 0:T], [(Wu[:, k, cs_], hT[:, k, :]) for k in range(DT)], reads=["hT", wuk], writes=[bUk])
                    b = 0
                    P.op("act", (lambda bG, b: lambda e: e.activation(out=fsg[b][:], in_=bG[:, 0:T], func=AF.Silu))(bG, b),
                         reads=[bGk, f"fsg{b}"], writes=[f"fsg{b}"])
                    av, avk = actT(f)
                    P.op("dve", (lambda bU, b, av: lambda e: e.tensor_tensor(out=av, in0=fsg[b][:], in1=bU[:, 0:T], op=ALU.mult))(bU, b, av),
                         reads=[bUk, f"fsg{b}", avk], writes=[avk])
            if j + 1 < NT:
                norm1(j + 1)
            for hf in range(NHD):
                if j + 1 < NT and hf == NHD - 1:
                    to_feat(D, hT, "hT", DT)
                banks = [nextF() for _ in range(NCH)]
                for i, (f0, n) in enumerate(dch):
                    W, wk = wchunk(j, f"wd{hf}_{i}")
                    for cc in range(NCH):
                        bk, bkk = banks[cc]
                        pairs, rk = [], set()
                        for ii in range(n):
                            av, avk = actT(f0 + ii)
                            pairs.append((av[:, cc * 128:(cc + 1) * 128], W[:, ii, :]))
                            rk.add(avk)
                        mm_group(bk[:, 0:CWD], pairs, reads=sorted(rk) + [wk], writes=[bkk], first=(i == 0), last=(i == len(dch) - 1))
                for cc in range(NCH):
                    bk, bkk = banks[cc]
                    P.op("dve", (lambda cc, bk, hf, xt: lambda e: e.tensor_tensor(out=xt[:, cc, hf * CWD:(hf + 1) * CWD], in0=xt[:, cc, hf * CWD:(hf + 1) * CWD],
                                                                                   in1=bk[:, 0:CWD], op=ALU.add))(cc, bk, hf, xt),
                         reads=[bkk, xk[cc]], writes=[xk[cc]])
            rmsnorm(gf_d, xt, xk, (lambda xt: lambda cc: xt[:, cc, :])(xt), xk)
            tok = P.op("sp", (lambda r0, xt: lambda e: e.dma_start(out=out_d[r0:r0 + T, :].rearrange("(c p) d -> p c d", p=128), in_=xt[:]))(r0, xt),
                       reads=xk, writes=[f"out_d{j % 2}"], dsem=d_outs[j % 2], store=True)
            store_toks[j % 2] = tok
        P.final_wait("sp", [t_ for t_ in store_toks if t_ is not None])
        P.emit()
    return nc


def host_inputs(cfg, x, norm1_g, w_in, conv_qk, b_igate, b_fgate, mh_norm_g, w_pool, pool_scale,
                w_branch_a, w_branch_b, b_gate, w_out, norm2_g, w_ffn_gate, w_ffn_up, w_ffn_down, final_norm_g):
    c = derived(cfg)
    f = lambda a: np.ascontiguousarray(np.asarray(a, dtype=np.float32))
    H = c["H"]
    prm = np.concatenate([f(conv_qk)[0].reshape(-1, 128), f(b_gate)[0].reshape(-1, 128), f(pool_scale)[0].reshape(-1, 128)], axis=0)
    gb = np.zeros((1, 128), np.float32)
    gb[0, 0:H] = f(b_igate)[0]
    gb[0, H:2 * H] = f(b_fgate)[0]
    shared = {
        "w_in": f(w_in)[0], "w_pool": f(w_pool)[0], "w_a": f(w_branch_a)[0], "w_b": f(w_branch_b)[0], "w_out": f(w_out)[0],
        "wg": f(w_ffn_gate)[0], "wu": f(w_ffn_up)[0], "wd": f(w_ffn_down)[0],
        "g1": f(norm1_g)[0:1], "g2": f(norm2_g)[0:1], "gf": f(final_norm_g).reshape(1, -1), "mhg": f(mh_norm_g)[0:1],
        "prm": f(prm), "gb": gb,
    }
    return shared


def kernel(**inputs):
    cfg = FULL
    x = np.asarray(inputs["x"], dtype=np.float32)
    B, S, D = x.shape
    n = 8
    per = B // n
    shared = host_inputs(cfg, **inputs)
    nc = build(cfg)
    in_maps = []
    for i in range(n):
        m = dict(shared)
        m["x"] = np.ascontiguousarray(x[i * per:(i + 1) * per].reshape(per * S, D))
        in_maps.append(m)
    res = run_bass_kernel_spmd(nc, in_maps, core_ids=list(range(n)))
    out = np.concatenate([np.asarray(r["out"]).reshape(per, S, D) for r in res.results], axis=0)
    return out.astype(np.float32)
```
